# Optimizing a Trainium2 kernel written in Bass

```python
import jax, jax.numpy as jnp
from jax import lax
import numpy as np

D_MODEL = 2048
BATCH = 1
SEQ = 8192
DEPTH = 1
DEC_BATCH = 4
DEC_SEQ = 2048
PAST_LEN = 128

N_MEM = 256
GRID_W = 64
BLOCK_Q = 128
ROPE_THETA = 10000.0
EPS = 1e-6
HEADS_A = 8
KV_HEADS_A = 2
HEAD_DIM_A = 128
HEADS_B = 8
Q_LORA_B = 512
KV_LORA_B = 256
NOPE_DIM_B = 128
ROPE_DIM_B = 64
V_DIM_B = 128
HEADS_M = 4
HEAD_DIM_M = 128
N_BRANCH = 3
D_FF = 4 * D_MODEL
IN_WIDTHS = (
    HEADS_A * HEAD_DIM_A,
    KV_HEADS_A * HEAD_DIM_A,
    KV_HEADS_A * HEAD_DIM_A,
    Q_LORA_B,
    KV_LORA_B,
    ROPE_DIM_B,
    HEADS_M * HEAD_DIM_M,
    N_BRANCH * D_MODEL,
)
IN_WIDTH = (HEADS_A * HEAD_DIM_A + 2 * KV_HEADS_A * HEAD_DIM_A + Q_LORA_B + KV_LORA_B
            + ROPE_DIM_B + HEADS_M * HEAD_DIM_M + N_BRANCH * D_MODEL)

kernel_name = "hybrid_gqa_mla_memory_encoder"


def rmsnorm(x, g):
    x32 = x.astype(jnp.float32)
    y = x32 * lax.rsqrt(jnp.mean(x32 * x32, axis=-1, keepdims=True) + EPS)
    return (y * g.astype(jnp.float32)).astype(x.dtype)


def split_cols(z, widths):
    outs, start = [], 0
    for w in widths:
        outs.append(z[..., start:start + w])
        start += w
    return outs


def grid_positions(n_tokens):
    n_rows = n_tokens // GRID_W
    rows = jnp.repeat(jnp.arange(n_rows, dtype=jnp.int32), GRID_W)
    cols = jnp.tile(jnp.arange(GRID_W, dtype=jnp.int32), n_rows)
    return rows, cols


def rope_cos_sin(pos, dim):
    inv_freq = ROPE_THETA ** (-jnp.arange(0, dim, 2, dtype=jnp.float32) / dim)
    ang = pos.astype(jnp.float32)[:, None] * inv_freq[None, :]
    return jnp.cos(ang), jnp.sin(ang)


def apply_rope_1d(x, cos, sin):
    half = x.shape[-1] // 2
    x1, x2 = x[..., :half], x[..., half:]
    c = cos[:, None, :].astype(x.dtype)
    s = sin[:, None, :].astype(x.dtype)
    return jnp.concatenate([x1 * c - x2 * s, x2 * c + x1 * s], axis=-1)


def axial_rope(x, rows, cols):
    h = x.shape[-1] // 2
    cr, sr = rope_cos_sin(rows, h)
    cc, sc = rope_cos_sin(cols, h)
    return jnp.concatenate([apply_rope_1d(x[..., :h], cr, sr),
                            apply_rope_1d(x[..., h:], cc, sc)], axis=-1)


def blockwise_attention(q, k, v):
    b, sq, hk, g, dq = q.shape
    dv = v.shape[-1]
    nb = sq // BLOCK_Q
    scale = dq ** -0.5
    qb = q.reshape(b, nb, BLOCK_Q, hk, g, dq).transpose(1, 0, 2, 3, 4, 5)

    def attend(q_blk):
        s = jnp.einsum("bqhgd,bkhd->bhgqk", q_blk, k).astype(jnp.float32) * scale
        p = jax.nn.softmax(s, axis=-1).astype(v.dtype)
        return jnp.einsum("bhgqk,bkhd->bqhgd", p, v)

    o = lax.map(attend, qb)
    return o.transpose(1, 0, 2, 3, 4, 5).reshape(b, sq, hk * g * dv)


def mixer_block(x, mem, rows, cols, g_mix, w_in, g_qa, g_ka, g_cq, w_q_b, g_ckv, w_kv_b,
                g_mem, w_mem_kv, w_br_a, w_br_b, w_br_m, w_out):
    b, s, _ = x.shape
    n = rmsnorm(x, g_mix)
    z = n @ w_in
    q_a, k_a, v_a, c_q, c_kv, k_rope, q_m, gate_logits = split_cols(z, IN_WIDTHS)

    q_a = axial_rope(rmsnorm(q_a.reshape(b, s, HEADS_A, HEAD_DIM_A), g_qa), rows, cols)
    k_a = axial_rope(rmsnorm(k_a.reshape(b, s, KV_HEADS_A, HEAD_DIM_A), g_ka), rows, cols)
    v_a = v_a.reshape(b, s, KV_HEADS_A, HEAD_DIM_A)
    q_a = q_a.reshape(b, s, KV_HEADS_A, HEADS_A // KV_HEADS_A, HEAD_DIM_A)
    o_a = blockwise_attention(q_a, k_a, v_a)

    q_b = (rmsnorm(c_q, g_cq) @ w_q_b).reshape(b, s, HEADS_B, NOPE_DIM_B + ROPE_DIM_B)
    q_b = jnp.concatenate([q_b[..., :NOPE_DIM_B],
                           axial_rope(q_b[..., NOPE_DIM_B:], rows, cols)], axis=-1)
    kv_b = (rmsnorm(c_kv, g_ckv) @ w_kv_b).reshape(b, s, HEADS_B, NOPE_DIM_B + V_DIM_B)
    k_nope, v_b = kv_b[..., :NOPE_DIM_B], kv_b[..., NOPE_DIM_B:]
    k_pe = axial_rope(k_rope.reshape(b, s, 1, ROPE_DIM_B), rows, cols)
    k_b = jnp.concatenate([k_nope, jnp.broadcast_to(k_pe, (b, s, HEADS_B, ROPE_DIM_B))], axis=-1)
    o_b = blockwise_attention(q_b[:, :, :, None, :], k_b, v_b)

    n_mem = mem.shape[1]
    kv_m = (rmsnorm(mem, g_mem) @ w_mem_kv).reshape(b, n_mem, 2, HEADS_M, HEAD_DIM_M)
    o_m = blockwise_attention(q_m.reshape(b, s, HEADS_M, 1, HEAD_DIM_M),
                              kv_m[:, :, 0], kv_m[:, :, 1])

    gates = jax.nn.sigmoid(gate_logits.astype(jnp.float32)).astype(x.dtype)
    gates = gates.reshape(b, s, N_BRANCH, D_MODEL)
    merged = (gates[:, :, 0] * (o_a @ w_br_a)
              + gates[:, :, 1] * (o_b @ w_br_b)
              + gates[:, :, 2] * (o_m @ w_br_m))
    return merged @ w_out


def encoder_trunk(x, mem, g_mix, w_in, g_qa, g_ka, g_cq, w_q_b, g_ckv, w_kv_b, g_mem,
                  w_mem_kv, w_br_a, w_br_b, w_br_m, w_out, g_mlp, w_up, w_down, g_final):
    rows, cols = grid_positions(x.shape[1])
    h = x
    for l in range(DEPTH):
        h = h + mixer_block(h, mem, rows, cols, g_mix[l], w_in[l], g_qa[l], g_ka[l], g_cq[l],
                            w_q_b[l], g_ckv[l], w_kv_b[l], g_mem[l], w_mem_kv[l],
                            w_br_a[l], w_br_b[l], w_br_m[l], w_out[l])
        n = rmsnorm(h, g_mlp[l])
        h = h + jnp.square(jax.nn.relu(n @ w_up[l])) @ w_down[l]
    return rmsnorm(h, g_final)


def setup_inputs(seed: int = 0) -> dict:
    key = jax.random.key(seed)
    ks = jax.random.split(key, 24)
    f32 = jnp.float32

    def w(k, fan_in, fan_out):
        return jax.random.normal(k, (DEPTH, fan_in, fan_out), f32) * fan_in ** -0.5

    def gain(k, dim):
        return 1.0 + 0.02 * jax.random.normal(k, (DEPTH, dim), f32)

    return {
        "x_prompt": jax.random.normal(ks[0], (BATCH, SEQ, D_MODEL), f32),
        "x_sample": jax.random.normal(ks[1], (DEC_BATCH, DEC_SEQ, D_MODEL), f32),
        "mem_prompt": jax.random.normal(ks[2], (BATCH, N_MEM, D_MODEL), f32),
        "mem_sample": jax.random.normal(ks[3], (DEC_BATCH, N_MEM, D_MODEL), f32),
        "g_mix": gain(ks[4], D_MODEL),
        "w_in": w(ks[5], D_MODEL, IN_WIDTH),
        "g_qa": gain(ks[6], HEAD_DIM_A),
        "g_ka": gain(ks[7], HEAD_DIM_A),
        "g_cq": gain(ks[8], Q_LORA_B),
        "w_q_b": w(ks[9], Q_LORA_B, HEADS_B * (NOPE_DIM_B + ROPE_DIM_B)),
        "g_ckv": gain(ks[10], KV_LORA_B),
        "w_kv_b": w(ks[11], KV_LORA_B, HEADS_B * (NOPE_DIM_B + V_DIM_B)),
        "g_mem": gain(ks[12], D_MODEL),
        "w_mem_kv": w(ks[13], D_MODEL, 2 * HEADS_M * HEAD_DIM_M),
        "w_br_a": w(ks[14], HEADS_A * HEAD_DIM_A, D_MODEL),
        "w_br_b": w(ks[15], HEADS_B * V_DIM_B, D_MODEL),
        "w_br_m": w(ks[16], HEADS_M * HEAD_DIM_M, D_MODEL),
        "w_out": w(ks[17], D_MODEL, D_MODEL),
        "g_mlp": gain(ks[18], D_MODEL),
        "w_up": w(ks[19], D_MODEL, D_FF),
        "w_down": w(ks[20], D_FF, D_MODEL),
        "g_final": 1.0 + 0.02 * jax.random.normal(ks[21], (D_MODEL,), f32),
    }


def reference(x_prompt, x_sample, mem_prompt, mem_sample, g_mix, w_in, g_qa, g_ka, g_cq,
              w_q_b, g_ckv, w_kv_b, g_mem, w_mem_kv, w_br_a, w_br_b, w_br_m, w_out,
              g_mlp, w_up, w_down, g_final):
    weights = (g_mix, w_in, g_qa, g_ka, g_cq, w_q_b, g_ckv, w_kv_b, g_mem, w_mem_kv,
               w_br_a, w_br_b, w_br_m, w_out, g_mlp, w_up, w_down, g_final)
    y_prompt = encoder_trunk(x_prompt, mem_prompt, *weights)
    y_sample = encoder_trunk(x_sample, mem_sample, *weights)
    return (y_prompt, y_sample)
```

```python
import numpy as np
import concourse.bass as bass
import concourse.mybir as mybir
from concourse.bass_utils import run_bass_kernel_spmd

F32 = mybir.dt.float32
BF16 = mybir.dt.bfloat16
ALU = mybir.AluOpType
AF = mybir.ActivationFunctionType

ENGS = ("pe", "act", "dve", "pool", "sp")
D = 2048
NCORE = 8
LP, LS = 8192, 2048
LC = LP + LS
TQ = 512
NJOB = 4
EPS = 1e-6
O_QA, O_KA, O_VA, O_CQ, O_CKV, O_KR, O_QM, O_G = 0, 1024, 1280, 1536, 2048, 2304, 2368, 2880
G_MIX, G_MLP, G_FIN, G_MEM, G_QA, G_KA, G_CQ, G_CKV, G_EPS, G_MLO, G_MHI, NG = 0, 16, 32, 48, 64, 65, 66, 70, 72, 73, 74, 75
SLOTW = 6144
DEBUG = False
KNOB = {"ntail": 14, "n1": 41, "lag": 4, "pool_den": True, "s5": 9, "s1": 99, "nctx": LC // 512, "mem": True, "njob": NJOB, "phase": "g"}


class Buf:
    def __init__(self, name, ap=None, accum=False):
        self.name = name
        self.ap = ap
        self.w = {}
        self.r = {}
        self.accum = accum
        self.dsem = None
        self.psum = False
        self.aliases = []


class Tracker:
    def __init__(self, nc):
        self.nc = nc
        self.q = {e: [] for e in ENGS}
        self.cnt = {e: 0 for e in ENGS}
        self.waited = {e: {} for e in ENGS}
        self.dtotal = {}
        self.sems = {}
        self.ndsem = 0
        self.self_wait = True

    def sem(self, name):
        if name not in self.sems:
            self.sems[name] = self.nc.alloc_semaphore(name)
        return self.sems[name]

    def _dsem_for(self, buf, queue):
        kind = "sw" if queue == "pool" else "hw"
        if buf.dsem is None:
            buf.dsem = {}
        if kind not in buf.dsem:
            name = "d%d" % self.ndsem
            self.ndsem += 1
            self.dtotal[name] = 0
            self.sem(name)
            buf.dsem[kind] = name
        return buf.dsem[kind]

    def _wait(self, eng, toks):
        for s, v in toks:
            if s in self.dtotal:
                pass
            else:
                if s == eng and (eng == "pe" or not self.self_wait):
                    continue
                assert v <= self.cnt[s], (
                    "wait on unissued milestone %s %d > %d (eng %s)" % (s, v, self.cnt[s], eng))
            if v <= 0 or self.waited[eng].get(s, 0) >= v:
                continue
            self.waited[eng][s] = v
            self.q[eng].append(("wait", s, v))

    def _deps(self, reads, writes, eng=None):
        toks = []
        for b in reads:
            toks.extend(b.w.items())
            if b.psum:
                toks.extend(b.r.items())
        for b in writes:
            toks.extend(b.w.items())
            toks.extend(b.r.items())
        return toks

    @staticmethod
    def _expand(bufs):
        return [x for b in bufs for x in [b] + b.aliases]

    def op(self, eng, fn, reads=(), writes=(), inc=True):
        reads, writes = self._expand(reads), self._expand(writes)
        self._wait(eng, self._deps(reads, writes, eng))
        if inc:
            self.cnt[eng] += 1
            v = self.cnt[eng]
        else:
            v = self.cnt[eng] + 1
        self.q[eng].append(("op", fn, eng if inc else None, 1))
        for b in reads:
            b.r[eng] = max(b.r.get(eng, 0), v)
        for b in writes:
            if b.accum:
                b.w[eng] = max(b.w.get(eng, 0), v)
            else:
                b.w = {eng: v}
                b.r = {}

    def dma(self, queue, out_ap, in_ap, reads=(), writes=(), sem_buf=None):
        reads, writes = self._expand(reads), self._expand(writes)
        self._wait(queue, self._deps(reads, writes, queue))
        s = self._dsem_for(sem_buf, queue)
        self.dtotal[s] += 16
        v = self.dtotal[s]
        self.q[queue].append(("op", (lambda e, o=out_ap, i=in_ap: e.dma_start(out=o, in_=i)), s, 16))
        for b in reads:
            b.r[s] = v
        for b in writes:
            if b.accum:
                b.w[s] = v
            else:
                b.w = {s: v}
                b.r = {}

    def final_wait(self, eng="sp"):
        toks = [(e, self.cnt[e]) for e in ("pe", "act", "dve", "pool")]
        toks += [(s, v) for s, v in self.dtotal.items()]
        self._wait(eng, [t for t in toks if t[0] != eng])

    def emit(self):
        nc = self.nc
        for e in ("pe", "act", "dve", "pool"):
            self.sem(e)
        engmap = {"pe": "tensor", "act": "scalar", "dve": "vector", "pool": "gpsimd", "sp": "sync"}
        with nc.Block() as block:
            for e in ENGS:
                ops = self.q[e]

                def body(h, ops=ops):
                    for o in ops:
                        if o[0] == "wait":
                            h.wait_ge(self.sems[o[1]], o[2])
                        else:
                            ins = o[1](h)
                            if o[2] is not None:
                                ins.then_inc(self.sems[o[2]], o[3])

                getattr(block, engmap[e])(body)


class Ring:
    def __init__(self, bufs):
        self.bufs = bufs
        self.i = 0

    def next(self):
        b = self.bufs[self.i % len(self.bufs)]
        self.i += 1
        return b


def build_program():
    nc = bass.Bass("TRN2", target_bir_lowering=False)
    T = Tracker(nc)

    def din(name, shape):
        return nc.dram_tensor(name, list(shape), F32, kind="ExternalInput").ap()

    xq = din("xq", [D, NJOB * TQ])
    xc = din("xc", [D, LC])
    memT = din("memT", [D, 512])
    tabA_c = din("tabA_c", [128, 2, LC])
    tabB_c = din("tabB_c", [128, 2, LC])
    tabA_q = din("tabA_q", [128, 2, NJOB * TQ])
    tabB_q = din("tabB_q", [128, 2, NJOB * TQ])
    rmat = din("rmat", [128, 256])
    gcols_d = din("gcols", [128, NG])
    w_in = din("w_in", [D, 9024])
    w_q_b = din("w_q_b", [512, 1536])
    w_kv_b = din("w_kv_b", [256, 2048])
    w_mem_kv = din("w_mem_kv", [D, 1024])
    w_br_a = din("w_br_a", [1024, D])
    w_br_b = din("w_br_b", [1024, D])
    w_br_m = din("w_br_m", [512, D])
    w_out = din("w_out", [D, D])
    w_up = din("w_up", [D, 4 * D])
    w_down = din("w_down", [4 * D, D])
    yT = nc.dram_tensor("yT", [D, NJOB * TQ], F32, kind="ExternalOutput").ap()

    okind = "ExternalOutput" if DEBUG else "Internal"
    KA = nc.dram_tensor("KA", [2, 128, LC], BF16, kind=okind).ap()
    VA = nc.dram_tensor("VA", [2, 128, LC // 128, 128], BF16, kind=okind).ap()
    KB = nc.dram_tensor("KB", [8, 128, LC], BF16, kind=okind).ap()
    KPE = nc.dram_tensor("KPE", [2, 128, LC], BF16, kind=okind).ap()
    VB = nc.dram_tensor("VB", [8, 128, LC // 128, 128], BF16, kind=okind).ap()
    KA_b, VA_b, KB_b, KPE_b, VB_b = (Buf(n, accum=True) for n in ("KA", "VA", "KB", "KPE", "VB"))

    NWORDS = 53000
    arena = nc.alloc_sbuf_tensor("arena", [128, NWORDS], F32)
    apos = [0]

    def alloc_f(name, n):
        o = apos[0]
        apos[0] += n
        assert apos[0] <= NWORDS, "SBUF arena overflow at %s: %d" % (name, apos[0])
        b = Buf(name, arena[:, o:o + n])
        b.off = o
        return b

    def alloc_b(name, n):
        assert n % 2 == 0
        o = apos[0]
        apos[0] += n // 2
        assert apos[0] <= NWORDS, "SBUF arena overflow at %s: %d" % (name, apos[0])
        b = Buf(name, arena[:, o:o + n // 2].bitcast(BF16))
        b.off = o
        return b

    ps_t = [nc.alloc_psum_tensor("psb%d" % i, [128, 512], F32) for i in range(8)]
    PSB = [Buf("ps%d" % i, ps_t[i][:, :]) for i in range(8)]
    for b_ in PSB:
        b_.psum = True
    ps_acc = PSB[0:2]
    ps_ring = Ring(PSB[2:8])

    ones_b = alloc_b("ones_b", 128)
    ones_f = alloc_f("ones_f", 128)
    rm_b = alloc_b("rm_b", 256)
    gc = alloc_f("gcols", NG + 3)
    KM = alloc_b("KM", 2 * 4 * 256)
    VM = alloc_b("VM", 2 * 2 * 512)
    wring = Ring([alloc_b("wslot%d" % i, SLOTW) for i in range(3)])
    xring = Ring([alloc_f("xslot%d" % i, 4 * 512) for i in range(2)])
    sqring = Ring([alloc_b("sq%d" % i, 512) for i in range(4)])
    tmpf = Ring([alloc_f("tmpf%d" % i, 512) for i in range(5)])
    tmpb = Ring([alloc_b("tmpb%d" % i, 512) for i in range(3)])
    rstd_x = alloc_f("rstd_x", 512)
    rstd_x2 = alloc_f("rstd_x2", 512)
    rstd_t = alloc_f("rstd_t", 8)
    tabA = alloc_f("tabA", 1024)
    tabB = alloc_f("tabB", 1024)
    xg = alloc_b("xg", 16 * 512)
    base_pos = apos[0]

    def gcol(i):
        return gc.ap[:, i:i + 1]

    epsc = gc.ap[:, G_EPS:G_EPS + 1]

    def MM(psb, out_ap, lhsT, rhs, start, stop, reads, last):
        T.op("pe", lambda e: e.matmul(out_ap, lhsT, rhs, start=start, stop=stop),
             reads=reads, writes=[psb], inc=last)

    def ACT(out_ap, in_ap, func, reads, writes, **kw):
        T.op("act", lambda e: e.activation(out=out_ap, in_=in_ap, func=func, **kw), reads=reads, writes=writes)

    def TT(out_ap, in0, in1, op, reads, writes, eng="dve"):
        T.op(eng, lambda e: e.tensor_tensor(out=out_ap, in0=in0, in1=in1, op=op), reads=reads, writes=writes)

    def STT(out_ap, in0, scalar, in1, op0, op1, reads, writes):
        T.op("dve", lambda e: e.scalar_tensor_tensor(out=out_ap, in0=in0, scalar=scalar, in1=in1, op0=op0, op1=op1),
             reads=reads, writes=writes)

    def TS(out_ap, in0, s1, op0, reads, writes, s2=None, op1=None):
        if op1 is None:
            T.op("dve", lambda e: e.tensor_scalar(out=out_ap, in0=in0, scalar1=s1, scalar2=None, op0=op0),
                 reads=reads, writes=writes)
        else:
            T.op("dve", lambda e: e.tensor_scalar(out=out_ap, in0=in0, scalar1=s1, scalar2=s2, op0=op0, op1=op1),
                 reads=reads, writes=writes)

    def rsq(outb, out_ap, in_ap, in_bufs):
        ACT(out_ap, in_ap, AF.Ln, reads=in_bufs + [gc], writes=[outb], bias=epsc, scale=1.0)
        ACT(out_ap, out_ap, AF.Exp, reads=[outb], writes=[outb], scale=-0.5)

    def cp_evac(k, out_ap, in_ap, reads, writes):
        if k % 2 == 0:
            ACT(out_ap, in_ap, AF.Copy, reads=reads, writes=writes)
        else:
            T.op("dve", lambda e: e.tensor_copy(out=out_ap, in_=in_ap), reads=reads, writes=writes)

    T.op("dve", lambda e: e.memset(ones_b.ap, 1.0), writes=[ones_b])
    T.op("dve", lambda e: e.memset(ones_f.ap, 1.0), writes=[ones_f])
    T.dma("pool", rm_b.ap, rmat, writes=[rm_b], sem_buf=rm_b)
    T.dma("sp", gc.ap[:, 0:NG], gcols_d, writes=[gc], sem_buf=gc)
    RA = rm_b.ap[:, 0:128]
    RB = rm_b.ap[:, 128:256]

    def rope(xb_buf, xb_ap, tab_buf, c_ap, s_ap, R, out_ap, out_bufs, n=512):
        pr = ps_ring.next()
        MM(pr, pr.ap[:, 0:n], R, xb_ap, True, True, [rm_b, xb_buf], True)
        t1 = tmpf.next()
        TT(t1.ap[:, 0:n], xb_ap, c_ap, ALU.mult, [xb_buf, tab_buf], [t1])
        t2 = tmpf.next()
        TT(t2.ap[:, 0:n], pr.ap[:, 0:n], s_ap, ALU.mult, [pr, tab_buf], [t2])
        TT(out_ap, t1.ap[:, 0:n], t2.ap[:, 0:n], ALU.add, [t1, t2], out_bufs)

    cur = {}

    def headnorm_part1(ps, gidx):
        rx, rx2 = cur["rstd_x"], cur["rstd_x2"]
        sq = sqring.next()
        ACT(sq.ap, ps.ap, AF.Square, [ps], [sq], scale=float(128 ** -0.5))
        pss = ps_ring.next()
        MM(pss, pss.ap, ones_b.ap, sq.ap, True, True, [ones_b, sq], True)
        u = tmpf.next()
        TT(u.ap, pss.ap, rx2.ap, ALU.mult, [pss, rx2], [u])
        rsq(u, u.ap, u.ap, [u])
        TT(u.ap, u.ap, rx.ap, ALU.mult, [u, rx], [u])
        qn = tmpb.next()
        STT(qn.ap, ps.ap, gcol(gidx), u.ap, ALU.mult, ALU.mult, [ps, gc, u], [qn])
        return qn

    def load_x_group(src, col0, ncol, g):
        xs = cur.get("xring", xring).next()
        v = xs.ap[:, 0:4 * ncol].rearrange("p (a n) -> p a n", n=ncol)
        T.dma("sp", v, src[g * 512:(g + 1) * 512, col0:col0 + ncol].rearrange("(a p) n -> p a n", p=128),
              writes=[xs], sem_buf=xs)
        return xs, v

    def build_xg(src, col0, ncol, gbase):
        xgb, rx, rx2 = cur["xg"], cur["rstd_x"], cur["rstd_x2"]
        pss = ps_ring.next()
        for g in range(4):
            xs, v = load_x_group(src, col0, ncol, g)
            for j in range(4):
                kc = 4 * g + j
                sq = sqring.next()
                ACT(sq.ap[:, 0:ncol], v[:, j, :], AF.Square, [xs], [sq], scale=float(D ** -0.5))
                MM(pss, pss.ap[:, 0:ncol], ones_b.ap, sq.ap[:, 0:ncol], kc == 0, kc == 15, [ones_b, sq], True)
                TS(xgb.ap[:, kc * 512:kc * 512 + ncol], v[:, j, :], gcol(gbase + kc), ALU.mult, [xs, gc], [xgb])
        rsq(rx, rx.ap[:, 0:ncol], pss.ap[:, 0:ncol], [pss])
        TT(rx2.ap[:, 0:ncol], rx.ap[:, 0:ncol], rx.ap[:, 0:ncol], ALU.mult, [rx], [rx2])

    def tok_rstd(ntt):
        rx, rt = cur["rstd_x"], cur["rstd_t"]
        pt = ps_ring.next()
        for tt in range(ntt):
            MM(pt, pt.ap[:, tt:tt + 1], rx.ap[0:1, tt * 128:(tt + 1) * 128], ones_f.ap[0:1, 0:1],
               True, True, [rx, ones_f], tt == ntt - 1)
        T.op("dve", lambda e: e.tensor_copy(out=rt.ap[:, 0:ntt], in_=pt.ap[:, 0:ntt]), reads=[pt], writes=[rt])

    m1 = apos[0]
    wkv = alloc_b("wkv", 16 * 896)
    wkvb = alloc_b("wkvb", 2 * 2048)
    craw = alloc_f("craw", 2 * 512)
    ckvn = alloc_b("ckvn", 2 * 512)
    ka_st = alloc_b("ka_st", 2 * 512)
    va_st = alloc_b("va_st", 2 * 4 * 128)
    kb_st = alloc_b("kb_st", 8 * 512)
    vb_st = alloc_b("vb_st", 8 * 4 * 128)
    kpe_st = alloc_b("kpe_st", 512)
    kpe2_st = alloc_b("kpe2_st", 2 * 512)
    xg2 = alloc_b("xg2", 16 * 512)
    rstd_xb = alloc_f("rstd_xb", 512)
    rstd_x2b = alloc_f("rstd_x2b", 512)
    rstd_tb = alloc_f("rstd_tb", 8)
    nxs = min(4, (NWORDS - apos[0]) // 2048)
    xring_s1 = Ring(xring.bufs + [alloc_f("xslot_s1_%d" % i, 4 * 512) for i in range(nxs)])
    CUR = [dict(xg=xg, rstd_x=rstd_x, rstd_x2=rstd_x2, rstd_t=rstd_t),
           dict(xg=xg2, rstd_x=rstd_xb, rstd_x2=rstd_x2b, rstd_t=rstd_tb)]
    cur.update(CUR[0])

    wkv_v = wkv.ap.rearrange("p (k n) -> p k n", n=896)
    for (dst0, src0, n) in ((0, O_KA, 256), (256, O_VA, 256), (512, O_CKV, 256), (768, O_KR, 64), (832, O_KR, 64)):
        T.dma("pool", wkv_v[:, :, dst0:dst0 + n], w_in[:, src0:src0 + n].rearrange("(k p) n -> p k n", p=128),
              writes=[wkv], sem_buf=wkv)
    wkvb_v = wkvb.ap.rearrange("p (k n) -> p k n", n=2048)
    T.dma("pool", wkvb_v, w_kv_b.rearrange("(k p) n -> p k n", p=128), writes=[wkvb], sem_buf=wkvb)
    wkvb_h = wkvb.ap.rearrange("p (k h n) -> p k h n", k=2, h=8)
    xg_v = xg.ap.rearrange("p (k n) -> p k n", n=512)

    def fence():
        wsems = set(n for b in wring.bufs for n in (b.dsem or {}).values())
        toks = [(e, T.cnt[e]) for e in ("pe", "act", "dve", "pool")] + [(k, v) for k, v in T.dtotal.items() if k not in wsems]
        for e in ENGS:
            T._wait(e, [t for t in toks if t[0] != e])

    s1_end = apos[0]
    apos[0] = m1
    oT = alloc_b("oT", 20 * 512)
    gates = [alloc_f("gate%d" % i, 512) for i in range(3)]
    rstd2 = alloc_f("rstd2", 512)
    yring = Ring([alloc_f("yst%d" % i, 512) for i in range(2)])
    z0 = apos[0]
    hbuf = alloc_f("hbuf", 16 * 512)
    hg = alloc_b("hg", 16 * 512)
    merged = alloc_b("merged", 16 * 512)
    z1 = apos[0]
    apos[0] = z1 - 4096
    aq = alloc_b("aq", 16 * 512)
    apos[0] = z0
    qa = alloc_b("qa", 8 * 512)
    qpe = alloc_b("qpe", 4 * 512)
    cqn = alloc_b("cqn", 4 * 512)
    cqraw = alloc_f("cqraw", 4 * 512)
    NACC = 4
    daccs = [alloc_f("dacc%d" % i, 512) for i in range(NACC)]
    rden = alloc_f("rden", 512)
    kvring = Ring([(alloc_b("kc%d" % i, 1024), alloc_b("pc%d" % i, 1024), alloc_b("vc%d" % i, 1024)) for i in range(3)])
    pring = Ring([alloc_b("pT%d" % i, 512) for i in range(6)])
    assert apos[0] <= z1, (apos[0], z1)
    apos[0] = z1
    qa_v = qa.ap.rearrange("p (h n) -> p h n", n=512)
    qpe_v = qpe.ap.rearrange("p (h n) -> p h n", n=512)
    cqn_v = cqn.ap.rearrange("p (h n) -> p h n", n=512)
    oT_v = oT.ap.rearrange("p (h n) -> p h n", n=512)
    mg_v = merged.ap.rearrange("p (h n) -> p h n", n=512)
    hb_v = hbuf.ap.rearrange("p (h n) -> p h n", n=512)
    hg_v = hg.ap.rearrange("p (h n) -> p h n", n=512)
    aq_v = aq.ap.rearrange("p (h n) -> p h n", n=512)

    tasks = []

    jl_cnt = [0]
    cur_idx = [0]
    WMODE = ["convert"]
    nused_tab = {}

    def wtask(load, compute):
        if load is not None:
            idx = jl_cnt[0]
            jl_cnt[0] += 1

            def load2(ws, idx=idx, load=load):
                cur_idx[0] = idx
                load(ws)
            tasks.append((load2, compute))
        else:
            tasks.append((None, compute))

    def run_tasks(depth=2):
        n = len(tasks)
        slots = [None] * n
        lidx = [i for i in range(n) if tasks[i][0] is not None]
        nxt = 0
        for i in range(n):
            while nxt < len(lidx) and (lidx[nxt] <= i or sum(1 for j in lidx[:nxt] if j > i) < depth):
                k = lidx[nxt]
                slots[k] = wring.next()
                tasks[k][0](slots[k])
                nxt += 1
            tasks[i][1](slots[i])
        tasks.clear()

    def wload(ws, views, nused):
        if WMODE[0] == "convert":
            for d, s in views:
                T.dma("pool", d, s, writes=[ws], sem_buf=ws)
            nused_tab[cur_idx[0]] = nused
        else:
            i = cur_idx[0]
            T.dma("sp", ws.ap[:, 0:nused], Wc[i, :, 0:nused], reads=[Wc_bufs[i]], writes=[ws], sem_buf=ws)

    def proj16(wsrc, col0, ntile, consume, rhs_v, rhs_buf):
        i = 0
        pend_def = []
        while i < ntile:
            n = min(3, ntile - i)

            def load(ws, i=i, n=n):
                wv = ws.ap[:, 0:16 * n * 128].rearrange("p (k n) -> p k n", n=n * 128)
                wload(ws, [(wv, wsrc[:, col0 + i * 128:col0 + (i + n) * 128].rearrange("(k p) n -> p k n", p=128))], 16 * n * 128)

            def comp(ws, i=i, n=n):
                wv = ws.ap[:, 0:16 * n * 128].rearrange("p (k n) -> p k n", n=n * 128)
                for jl in range(n):
                    ps = ps_ring.next()
                    for kc in range(16):
                        MM(ps, ps.ap, wv[:, kc, jl * 128:(jl + 1) * 128], rhs_v[:, kc, :], kc == 0, kc == 15,
                           [ws, rhs_buf], kc == 15)
                    t = i + jl
                    due = sorted([e for e in pend_def if e[0] <= t], key=lambda e: e[0])
                    for e in due:
                        pend_def.remove(e)
                        e[1]()
                    d = consume(t, ps)
                    if d is not None:
                        if callable(d):
                            d = [(1, d)]
                        for delay, fn in d:
                            pend_def.append((t + delay, fn))
                if i + n >= ntile:
                    for e in sorted(pend_def, key=lambda e: e[0]):
                        e[1]()
                    del pend_def[:]

            wtask(load, comp)
            i += n

    def attention(kind, jb, c0, L, mj):
        nh = {"A": 8, "B": 8, "M": 4}[kind]
        obase = {"A": 0, "B": 8, "M": 16}[kind]
        scale = {"A": 128 ** -0.5, "B": 192 ** -0.5, "M": 128 ** -0.5}[kind]
        nch = 1 if kind == "M" else L // 1024
        steps = [(h, ch) for h in range(nh) for ch in range(nch)]
        loaded = {}

        def issue(si):
            h, ch = steps[si]
            kc_, pc_, vc_ = kvring.next()
            t0 = c0 + ch * 1024
            if kind == "A":
                T.dma("sp", kc_.ap, KA[h // 4, :, t0:t0 + 1024], reads=[KA_b], writes=[kc_], sem_buf=kc_)
                T.dma("sp", vc_.ap.rearrange("p (t d) -> p t d", d=128), VA[h // 4, :, t0 // 128:t0 // 128 + 8, :],
                      reads=[VA_b], writes=[vc_], sem_buf=vc_)
            else:
                T.dma("sp", kc_.ap, KB[h, :, t0:t0 + 1024], reads=[KB_b], writes=[kc_], sem_buf=kc_)
                T.dma("sp", pc_.ap, KPE[h % 2, :, t0:t0 + 1024], reads=[KPE_b], writes=[pc_], sem_buf=pc_)
                T.dma("sp", vc_.ap.rearrange("p (t d) -> p t d", d=128), VB[h, :, t0 // 128:t0 // 128 + 8, :],
                      reads=[VB_b], writes=[vc_], sem_buf=vc_)
            loaded[si] = (kc_, pc_, vc_)

        nissued = 0
        pend = []
        nd = [0]
        for si, (h, ch) in enumerate(steps):
            if kind != "M":
                while nissued < min(len(steps), si + 2):
                    issue(nissued)
                    nissued += 1
                kc_, pc_, vc_ = loaded.pop(si)
            nkt = 2 if kind == "M" else 8
            pso = ps_acc[0]
            for kt in range(nkt):
                first = (ch == 0 and kt == 0)
                last = (ch == nch - 1 and kt == nkt - 1)
                pss = ps_ring.next()
                if kind == "A":
                    MM(pss, pss.ap, kc_.ap[:, kt * 128:(kt + 1) * 128], qa_v[:, h, :], True, True, [kc_, qa], True)
                elif kind == "B":
                    MM(pss, pss.ap, kc_.ap[:, kt * 128:(kt + 1) * 128], qa_v[:, h, :], True, False, [kc_, qa], False)
                    MM(pss, pss.ap, pc_.ap[:, kt * 128:(kt + 1) * 128], qpe_v[:, h // 2, :], False, True,
                       [pc_, qpe], True)
                else:
                    MM(pss, pss.ap, KM_v[:, mj, h, kt * 128:(kt + 1) * 128], qa_v[:, h, :], True, True, [KM, qa], True)
                pT = pring.next()
                ACT(pT.ap, pss.ap, AF.Exp, [pss], [pT], scale=float(scale))
                ti = ch * nkt + kt
                pe_every = {"A": 3, "B": 4, "M": 0}[kind]
                if pe_every and ti % pe_every == 0:
                    dmode = "pe"
                else:
                    dmode = "dve"
                    a = daccs[nd[0] % NACC]
                    if nd[0] < NACC:
                        T.op("dve", lambda e, o=a.ap, i_=pT.ap: e.tensor_copy(out=o, in_=i_), reads=[pT], writes=[a])
                    else:
                        TT(a.ap, a.ap, pT.ap, ALU.add, [a, pT], [a])
                    nd[0] += 1
                if kind == "M":
                    vl, vb_ = VM_v[:, mj, kt, h * 128:(h + 1) * 128], VM
                else:
                    vl, vb_ = vc_.ap[:, kt * 128:(kt + 1) * 128], vc_
                pend.append((vl, vb_, pT, first, last, dmode == "pe"))

                def do_pv(ent):
                    vl2, vb2, pT2, f2, l2, pe_den = ent
                    MM(pso, pso.ap, vl2, pT2.ap, f2, l2, [vb2, pT2], l2)
                    if pe_den:
                        MM(ps_acc[1], ps_acc[1].ap, ones_b.ap, pT2.ap, f2, False, [ones_b, pT2], False)
                if len(pend) > KNOB["lag"]:
                    do_pv(pend.pop(0))
                if last:
                    while pend:
                        do_pv(pend.pop(0))
                    used = min(nd[0], NACC)
                    if pe_every:
                        psd = ps_acc[1]
                        for k in range(used):
                            MM(psd, psd.ap, ones_f.ap, daccs[k].ap, False, k == used - 1, [ones_f, daccs[k]], k == used - 1)
                    else:
                        psd = ps_ring.next()
                        for k in range(used):
                            MM(psd, psd.ap, ones_f.ap, daccs[k].ap, k == 0, k == used - 1, [ones_f, daccs[k]], k == used - 1)
                    nd[0] = 0
                    T.op("dve", lambda e, o=rden.ap, i_=psd.ap: e.reciprocal(out=o, in_=i_), reads=[psd], writes=[rden])
                    TT(oT_v[:, obase + h, :], pso.ap, rden.ap, ALU.mult, [pso, rden], [oT])

    conv2 = []

    def make_job(jb):
        jl_cnt[0] = 0
        q0 = jb * TQ
        PH = KNOB["phase"]
        is_p = jb < 2
        c0, L, mj = (0, LP, 0) if is_p else (LP, LS, 1)

        def t_a(ws, q0=q0):
            fence()
            T.dma("sp", tabA.ap.rearrange("p (a n) -> p a n", n=512), tabA_q[:, :, q0:q0 + 512], writes=[tabA], sem_buf=tabA)
            T.dma("sp", tabB.ap.rearrange("p (a n) -> p a n", n=512), tabB_q[:, :, q0:q0 + 512], writes=[tabB], sem_buf=tabB)
            build_xg(xq, q0, 512, G_MIX)
        wtask(None, t_a)
        if jb == 0 and conv2:
            wtask(None, lambda ws: conv2.pop()())

        def c_qa(i, ps):
            rx, rx2 = cur["rstd_x"], cur["rstd_x2"]
            sq = sqring.next()
            ACT(sq.ap, ps.ap, AF.Square, [ps], [sq], scale=float(128 ** -0.5))
            raw = cqraw.ap[:, (i % 4) * 512:(i % 4 + 1) * 512]
            ACT(raw, ps.ap, AF.Copy, [ps], [cqraw])
            qn_ap = cqn_v[:, i % 4, :]

            def stage1():
                pss = ps_ring.next()
                MM(pss, pss.ap, ones_b.ap, sq.ap, True, True, [ones_b, sq], True)
                u = tmpf.next()
                TT(u.ap, pss.ap, rx2.ap, ALU.mult, [pss, rx2], [u])
                rsq(u, u.ap, u.ap, [u])
                TT(u.ap, u.ap, rx.ap, ALU.mult, [u, rx], [u])
                STT(qn_ap, raw, gcol(G_QA), u.ap, ALU.mult, ALU.mult, [cqraw, gc, u], [cqn])

            def stage2():
                rope(cqn, qn_ap, tabA, tabA.ap[:, 0:512], tabA.ap[:, 512:1024], RA, qa_v[:, i, :], [qa])
            return [(1, stage1), (3, stage2)]
        proj16(w_in, O_QA, 8, c_qa, xg_v, xg)
        wtask(None, lambda ws, jb=jb, c0=c0, L=L, mj=mj: attention("A", jb, c0, L, mj))

        cq_sq = []

        def c_cq(i, ps, cq_sq=cq_sq):
            sq = alloc_sq[i]
            ACT(sq.ap, ps.ap, AF.Square, [ps], [sq], scale=float(512 ** -0.5))
            ACT(cqraw.ap[:, i * 512:(i + 1) * 512], ps.ap, AF.Copy, [ps], [cqraw])
            if i == 3:
                pss = ps_ring.next()
                for k in range(4):
                    MM(pss, pss.ap, ones_b.ap, alloc_sq[k].ap, k == 0, k == 3, [ones_b, alloc_sq[k]], k == 3)
                u = tmpf.next()
                TT(u.ap, pss.ap, rstd_x2.ap, ALU.mult, [pss, rstd_x2], [u])
                rsq(u, u.ap, u.ap, [u])
                TT(u.ap, u.ap, rstd_x.ap, ALU.mult, [u, rstd_x], [u])
                for k in range(4):
                    STT(cqn_v[:, k, :], cqraw.ap[:, k * 512:(k + 1) * 512], gcol(G_CQ + k), u.ap, ALU.mult, ALU.mult,
                        [cqraw, gc, u], [cqn])
        alloc_sq = sqring.bufs
        proj16(w_in, O_CQ, 4, c_cq, xg_v, xg)

        for half in range(2):
            def load(ws, half=half):
                wv = ws.ap[:, 0:4 * 768].rearrange("p (k n) -> p k n", n=768)
                wpe = ws.ap[:, 3072:4096].rearrange("p (k j n) -> p k j n", k=4, j=2)
                views = [(wv, w_q_b[:, half * 768:(half + 1) * 768].rearrange("(k p) n -> p k n", p=128))]
                for hl in range(4):
                    h = half * 4 + hl
                    views.append((wpe[:, :, hl // 2, (hl % 2) * 64:(hl % 2) * 64 + 64],
                                  w_q_b[:, h * 192 + 128:h * 192 + 192].rearrange("(k p) n -> p k n", p=128)))
                wload(ws, views, 4096)

            def comp(ws, half=half):
                wv = ws.ap[:, 0:4 * 768].rearrange("p (k h n) -> p k h n", k=4, h=4)
                wpe = ws.ap[:, 3072:4096].rearrange("p (k j n) -> p k j n", k=4, j=2)
                for hl in range(4):
                    h = half * 4 + hl
                    ps = ps_ring.next()
                    for kc in range(4):
                        MM(ps, ps.ap, wv[:, kc, hl, 0:128], cqn_v[:, kc, :], kc == 0, kc == 3, [ws, cqn], kc == 3)
                    cp_evac(hl, qa_v[:, h, :], ps.ap, [ps], [qa])
                for jl in range(2):
                    ps = ps_ring.next()
                    for kc in range(4):
                        MM(ps, ps.ap, wpe[:, kc, jl, :], cqn_v[:, kc, :], kc == 0, kc == 3, [ws, cqn], kc == 3)
                    qr = tmpb.next()
                    cp_evac(jl, qr.ap, ps.ap, [ps], [qr])
                    rope(qr, qr.ap, tabB, tabB.ap[:, 0:512], tabB.ap[:, 512:1024], RB, qpe_v[:, half * 2 + jl, :], [qpe])
            wtask(load, comp)
        wtask(None, lambda ws, jb=jb, c0=c0, L=L, mj=mj: attention("B", jb, c0, L, mj))

        def c_qm(i, ps):
            TT(qa_v[:, i, :], ps.ap, rstd_x.ap, ALU.mult, [ps, rstd_x], [qa])
        proj16(w_in, O_QM, 4, c_qm, xg_v, xg)
        wtask(None, lambda ws, jb=jb, c0=c0, L=L, mj=mj: attention("M", jb, c0, L, mj))

        wtask(None, lambda ws: fence())
        for j in range(16):
            def loadg(ws, j=j):
                wv = ws.ap[:, 0:16 * 384].rearrange("p (k b n) -> p k b n", k=16, b=3)
                wload(ws, [(wv[:, :, b, :],
                            w_in[:, O_G + b * D + j * 128:O_G + b * D + (j + 1) * 128].rearrange("(k p) n -> p k n", p=128))
                           for b in range(3)], 16 * 384)

            def compg(ws, j=j):
                wv = ws.ap[:, 0:16 * 384].rearrange("p (k b n) -> p k b n", k=16, b=3)
                for b in range(3):
                    ps = ps_ring.next()
                    for kc in range(16):
                        MM(ps, ps.ap, wv[:, kc, b, :], xg_v[:, kc, :], kc == 0, kc == 15, [ws, xg], kc == 15)
                    t = tmpf.next()
                    TT(t.ap, ps.ap, rstd_x.ap, ALU.mult, [ps, rstd_x], [t])
                    ACT(gates[b].ap, t.ap, AF.Sigmoid, [t], [gates[b]])

            def loadb(ws, j=j):
                wv = ws.ap[:, 0:20 * 128].rearrange("p (k n) -> p k n", n=128)
                wload(ws, [(wv[:, 0:8, :], w_br_a[:, j * 128:(j + 1) * 128].rearrange("(k p) n -> p k n", p=128)),
                           (wv[:, 8:16, :], w_br_b[:, j * 128:(j + 1) * 128].rearrange("(k p) n -> p k n", p=128)),
                           (wv[:, 16:20, :], w_br_m[:, j * 128:(j + 1) * 128].rearrange("(k p) n -> p k n", p=128))], 20 * 128)

            def compb(ws, j=j):
                wv = ws.ap[:, 0:20 * 128].rearrange("p (k n) -> p k n", n=128)
                ms = []
                for b, (k0, nk) in enumerate(((0, 8), (8, 8), (16, 4))):
                    ps = ps_ring.next()
                    for kc in range(nk):
                        MM(ps, ps.ap, wv[:, k0 + kc, :], oT_v[:, k0 + kc, :], kc == 0, kc == nk - 1, [ws, oT], kc == nk - 1)
                    m = tmpf.next()
                    TT(m.ap, ps.ap, gates[b].ap, ALU.mult, [ps, gates[b]], [m])
                    ms.append(m)
                TT(ms[0].ap, ms[0].ap, ms[1].ap, ALU.add, [ms[0], ms[1]], [ms[0]])
                TT(mg_v[:, j, :], ms[0].ap, ms[2].ap, ALU.add, [ms[0], ms[2]], [merged])
            wtask(loadg, compg)
            wtask(loadb, compb)

        def c_wo(i, ps, q0=q0):
            xs = xring.next()
            T.dma("sp", xs.ap[:, 0:512], xq[i * 128:(i + 1) * 128, q0:q0 + 512], writes=[xs], sem_buf=xs)
            TT(hb_v[:, i, :], ps.ap, xs.ap[:, 0:512], ALU.add, [ps, xs], [hbuf])
            sq = sqring.next()
            ACT(sq.ap, hb_v[:, i, :], AF.Square, [hbuf], [sq], scale=float(D ** -0.5))
            MM(ps_acc[1], ps_acc[1].ap, ones_b.ap, sq.ap, i == 0, i == 15, [ones_b, sq], True)
            TS(hg_v[:, i, :], hb_v[:, i, :], gcol(G_MLP + i), ALU.mult, [hbuf, gc], [hg])
            if i == 15:
                rsq(rstd2, rstd2.ap, ps_acc[1].ap, [ps_acc[1]])
        proj16(w_out, 0, 16, c_wo, mg_v, merged)

        wtask(None, lambda ws: fence())
        for qd in range(4):
            def c_up(i, ps):
                t = tmpf.next()
                STT(t.ap, ps.ap, 0.0, rstd2.ap, ALU.max, ALU.mult, [ps, rstd2], [t])
                ACT(aq_v[:, i, :], t.ap, AF.Square, [t], [aq])
            proj16(w_up, qd * 2048, 16, c_up, hg_v, hg)

            def c_dn(i, ps):
                TT(hb_v[:, i, :], hb_v[:, i, :], ps.ap, ALU.add, [hbuf, ps], [hbuf])
            proj16(w_down[qd * 2048:(qd + 1) * 2048, :], 0, 16, c_dn, aq_v, aq)

        def t_g(ws, q0=q0):
            for i in range(16):
                sq = sqring.next()
                ACT(sq.ap, hb_v[:, i, :], AF.Square, [hbuf], [sq], scale=float(D ** -0.5))
                MM(ps_acc[1], ps_acc[1].ap, ones_b.ap, sq.ap, i == 0, i == 15, [ones_b, sq], True)
            rsq(rstd2, rstd2.ap, ps_acc[1].ap, [ps_acc[1]])
            for i in range(16):
                ys = yring.next()
                STT(ys.ap, hb_v[:, i, :], gcol(G_FIN + i), rstd2.ap, ALU.mult, ALU.mult, [hbuf, gc, rstd2], [ys])
                T.dma("act", yT[i * 128:(i + 1) * 128, q0:q0 + 512], ys.ap, reads=[ys], sem_buf=ys)
        wtask(None, t_g)


    KM_v = KM.ap.rearrange("p (j h t) -> p j h t", j=2, h=4)
    VM_v = VM.ap.rearrange("p (j t n) -> p j t n", j=2, t=2)
    for mj in range(2 if KNOB["mem"] else 0):
        build_xg(memT, mj * 256, 256, G_MEM)
        tok_rstd(2)
        for grp in range(4):
            ws = wring.next()
            wv = ws.ap[:, 0:16 * 256].rearrange("p (k n) -> p k n", n=256)
            T.dma("pool", wv, w_mem_kv[:, grp * 256:(grp + 1) * 256].rearrange("(k p) n -> p k n", p=128),
                  writes=[ws], sem_buf=ws)
            if grp < 2:
                for jl in range(2):
                    h = grp * 2 + jl
                    ps = ps_ring.next()
                    for kc in range(16):
                        MM(ps, ps.ap[:, 0:256], wv[:, kc, jl * 128:(jl + 1) * 128], xg_v[:, kc, 0:256], kc == 0, kc == 15,
                           [ws, xg], kc == 15)
                    TT(KM_v[:, mj, h, :], ps.ap[:, 0:256], rstd_x.ap[:, 0:256], ALU.mult, [ps, rstd_x], [KM])
            else:
                for tt in range(2):
                    ps = ps_ring.next()
                    for kc in range(16):
                        MM(ps, ps.ap[:, 0:256], xg_v[:, kc, tt * 128:(tt + 1) * 128], wv[:, kc, :], kc == 0, kc == 15,
                           [ws, xg], kc == 15)
                    TS(VM_v[:, mj, tt, (grp - 2) * 256:(grp - 1) * 256], ps.ap[:, 0:256], rstd_t.ap[:, tt:tt + 1], ALU.mult,
                       [ps, rstd_t], [VM])


    make_job(0)
    ltasks = [t for t in tasks if t[0] is not None]
    tasks.clear()
    Wc = nc.dram_tensor("Wc", [len(ltasks), 128, SLOTW], BF16).ap()
    Wc_bufs = [Buf("Wc%d" % i) for i in range(len(ltasks))]

    def wstore(i, ws):
        n = nused_tab[i]
        T.dma("pool", Wc[i, :, 0:n], ws.ap[:, 0:n], reads=[ws], writes=[Wc_bufs[i]], sem_buf=ws)

    apos[0] = max(apos[0], s1_end)
    cring = wring
    N1 = min(KNOB["n1"], len(ltasks))

    tail_slot = []

    def convert_range(i0, i1, ring):
        WMODE[0] = "convert"
        prev = None
        for i in range(i0, i1):
            if tail_slot and i >= i1 - KNOB["ntail"]:
                if prev is not None:
                    wstore(*prev)
                    prev = None
                ws = tail_slot[0]
                ltasks[i][0](ws)
                wstore(i, ws)
                continue
            ws = ring.next()
            ltasks[i][0](ws)
            if prev is not None:
                wstore(*prev)
            prev = (i, ws)
        if prev is not None:
            wstore(*prev)
        WMODE[0] = "cached"

    if KNOB["njob"] > 0:
        convert_range(0, N1, wring)
    assert xring.bufs[1].off == xring.bufs[0].off + 2048
    cvX = Buf("cvX", arena[:, xring.bufs[0].off:xring.bufs[0].off + SLOTW // 2].bitcast(BF16))
    cvX.aliases = list(xring.bufs)
    gl = gates + [rstd2] + yring.bufs
    assert all(gl[k + 1].off == gl[k].off + 512 for k in range(5))
    cvG = Buf("cvG", arena[:, gl[0].off:gl[0].off + SLOTW // 2].bitcast(BF16))
    cvG.aliases = gl
    cv_ring = Ring([cvX, cvG])
    conv2.append(lambda: (tail_slot.append(cvX), convert_range(N1, len(ltasks), cv_ring)))
    WMODE[0] = "cached"
    apos[0] = max(apos[0], s1_end)

    cur["xring"] = xring_s1
    for cb in range(KNOB["nctx"]):
        c0 = cb * 512
        cur.update(CUR[cb % 2])
        xgb, rx, rx2, rt = cur["xg"], cur["rstd_x"], cur["rstd_x2"], cur["rstd_t"]
        xgv = xgb.ap.rearrange("p (k n) -> p k n", n=512)
        build_xg(xc, c0, 512, G_MIX)
        T.dma("sp", tabA.ap.rearrange("p (a n) -> p a n", n=512), tabA_c[:, :, c0:c0 + 512], writes=[tabA], sem_buf=tabA)
        T.dma("sp", tabB.ap.rearrange("p (a n) -> p a n", n=512), tabB_c[:, :, c0:c0 + 512], writes=[tabB], sem_buf=tabB)
        qns = []
        for h in range(2):
            ps = ps_ring.next()
            for kc in range(16):
                MM(ps, ps.ap, wkv_v[:, kc, h * 128:(h + 1) * 128], xgv[:, kc, :], kc == 0, kc == 15, [wkv, xgb], kc == 15)
            qns.append(headnorm_part1(ps, G_KA))
        tok_rstd(4)
        va_v = va_st.ap.rearrange("p (h t d) -> p h t d", h=2, t=4)
        for tt in range(4):
            ps = ps_ring.next()
            for kc in range(16):
                MM(ps, ps.ap[:, 0:256], xgv[:, kc, tt * 128:(tt + 1) * 128], wkv_v[:, kc, 256:512], kc == 0, kc == 15,
                   [wkv, xgb], kc == 15)
            TS(va_v[:, :, tt, :], ps.ap[:, 0:256].rearrange("p (h d) -> p h d", h=2), rt.ap[:, tt:tt + 1], ALU.mult,
               [ps, rt], [va_st])
        T.dma("act", VA[:, :, 4 * cb:4 * cb + 4, :].rearrange("h p t d -> p h t d"), va_v,
              reads=[va_st], writes=[VA_b], sem_buf=va_st)
        sqs = []
        for i in range(2):
            ps = ps_ring.next()
            for kc in range(16):
                MM(ps, ps.ap, wkv_v[:, kc, 512 + i * 128:512 + (i + 1) * 128], xgv[:, kc, :], kc == 0, kc == 15,
                   [wkv, xgb], kc == 15)
            sq = sqring.next()
            ACT(sq.ap, ps.ap, AF.Square, [ps], [sq], scale=float(256 ** -0.5))
            ACT(craw.ap[:, i * 512:(i + 1) * 512], ps.ap, AF.Copy, [ps], [craw])
            sqs.append(sq)
        for h in range(2):
            rope(qns[h], qns[h].ap, tabA, tabA.ap[:, 0:512], tabA.ap[:, 512:1024], RA, ka_st.ap[:, h * 512:(h + 1) * 512], [ka_st])
        T.dma("act", KA[:, :, c0:c0 + 512].rearrange("h d t -> d h t"), ka_st.ap.rearrange("p (h t) -> p h t", h=2),
              reads=[ka_st], writes=[KA_b], sem_buf=ka_st)
        pss = ps_ring.next()
        for i in range(2):
            MM(pss, pss.ap, ones_b.ap, sqs[i].ap, i == 0, i == 1, [ones_b, sqs[i]], i == 1)
        u = tmpf.next()
        TT(u.ap, pss.ap, rx2.ap, ALU.mult, [pss, rx2], [u])
        rsq(u, u.ap, u.ap, [u])
        TT(u.ap, u.ap, rx.ap, ALU.mult, [u, rx], [u])
        for i in range(2):
            STT(ckvn.ap[:, i * 512:(i + 1) * 512], craw.ap[:, i * 512:(i + 1) * 512], gcol(G_CKV + i), u.ap, ALU.mult, ALU.mult,
                [craw, gc, u], [ckvn])
        ps = ps_ring.next()
        for kc in range(16):
            MM(ps, ps.ap, wkv_v[:, kc, 768:896], xgv[:, kc, :], kc == 0, kc == 15, [wkv, xgb], kc == 15)
        kr = tmpb.next()
        TT(kr.ap, ps.ap, rx.ap, ALU.mult, [ps, rx], [kr])
        ckvn_v = ckvn.ap.rearrange("p (k n) -> p k n", n=512)
        for h in range(8):
            ps = ps_ring.next()
            for kc in range(2):
                MM(ps, ps.ap, wkvb_h[:, kc, h, 0:128], ckvn_v[:, kc, :], kc == 0, kc == 1, [wkvb, ckvn], kc == 1)
            cp_evac(h, kb_st.ap[:, h * 512:(h + 1) * 512], ps.ap, [ps], [kb_st])
            if h == 1:
                rope(kr, kr.ap, tabB, tabB.ap[:, 0:512], tabB.ap[:, 512:1024], RB, kpe_st.ap, [kpe_st])
                TS(kpe2_st.ap[:, 0:512], kpe_st.ap, gcol(G_MLO), ALU.mult, [kpe_st, gc], [kpe2_st])
                TS(kpe2_st.ap[:, 512:1024], kpe_st.ap, gcol(G_MHI), ALU.mult, [kpe_st, gc], [kpe2_st])
                T.dma("act", KPE[:, :, c0:c0 + 512].rearrange("v p t -> p v t"), kpe2_st.ap.rearrange("p (v t) -> p v t", v=2),
                      reads=[kpe2_st], writes=[KPE_b], sem_buf=kpe2_st)
        T.dma("act", KB[:, :, c0:c0 + 512].rearrange("h d t -> d h t"), kb_st.ap.rearrange("p (h t) -> p h t", h=8),
              reads=[kb_st], writes=[KB_b], sem_buf=kb_st)
        vb_v = vb_st.ap.rearrange("p (h t d) -> p h t d", h=8, t=4)
        for tt in range(4):
            for hgi in range(2):
                ps = ps_ring.next()
                for kc in range(2):
                    MM(ps, ps.ap.rearrange("p (h d) -> p h d", h=4), ckvn_v[:, kc, tt * 128:(tt + 1) * 128],
                       wkvb_h[:, kc, 4 * hgi:4 * hgi + 4, 128:256], kc == 0, kc == 1, [wkvb, ckvn], kc == 1)
                cp_evac(tt * 2 + hgi + 1, vb_v[:, 4 * hgi:4 * hgi + 4, tt, :], ps.ap.rearrange("p (h d) -> p h d", h=4), [ps], [vb_st])
        T.dma("act", VB[:, :, 4 * cb:4 * cb + 4, :].rearrange("h p t d -> p h t d"), vb_v,
              reads=[vb_st], writes=[VB_b], sem_buf=vb_st)
    cur.update(CUR[0])
    cur["xring"] = xring

    fence()
    for jb in range(KNOB["njob"]):
        make_job(jb)
    run_tasks()
    T.final_wait("sp")
    T.emit()
    return nc


def _rope_tables(rows, cols, kind):
    d = np.arange(128)
    if kind == "A":
        part = d // 64
        i = d % 32
        f = (10000.0 ** (-(np.arange(0, 64, 2, dtype=np.float32)) / 64)).astype(np.float32)[i]
        first = (d % 64) < 32
    else:
        dd = d % 64
        part = dd // 32
        i = dd % 16
        f = (10000.0 ** (-(np.arange(0, 32, 2, dtype=np.float32)) / 32)).astype(np.float32)[i]
        first = (dd % 32) < 16
    pos = np.where(part[:, None] == 0, rows[None, :], cols[None, :]).astype(np.float32)
    ang = (pos * f[:, None]).astype(np.float32)
    c = np.cos(ang.astype(np.float64))
    s = np.sin(ang.astype(np.float64)) * np.where(first, -1.0, 1.0)[:, None]
    return np.ascontiguousarray(np.stack([c, s], axis=1).astype(np.float32))


def _rmats():
    d = np.arange(128)
    pa = np.where((d % 64) < 32, d + 32, d - 32)
    pb = np.where((d % 32) < 16, d + 16, d - 16)
    R = np.zeros((128, 256), np.float32)
    R[pa, d] = 1.0
    R[pb, 128 + d] = 1.0
    return R


_NC_CACHE = {}


def prep_inputs(x_prompt, x_sample, mem_prompt, mem_sample, g_mix, w_in, g_qa, g_ka, g_cq, w_q_b, g_ckv, w_kv_b,
                g_mem, w_mem_kv, w_br_a, w_br_b, w_br_m, w_out, g_mlp, w_up, w_down, g_final, cores=range(NCORE)):
    f = lambda a: np.ascontiguousarray(np.asarray(a, dtype=np.float32))
    x_prompt, x_sample, mem_prompt, mem_sample = f(x_prompt), f(x_sample), f(mem_prompt), f(mem_sample)
    gcols = np.zeros((128, NG), np.float32)
    gcols[:, G_MIX:G_MIX + 16] = f(g_mix).reshape(16, 128).T
    gcols[:, G_MLP:G_MLP + 16] = f(g_mlp).reshape(16, 128).T
    gcols[:, G_FIN:G_FIN + 16] = f(g_final).reshape(16, 128).T
    gcols[:, G_MEM:G_MEM + 16] = f(g_mem).reshape(16, 128).T
    gcols[:, G_QA] = f(g_qa).reshape(128)
    gcols[:, G_KA] = f(g_ka).reshape(128)
    gcols[:, G_CQ:G_CQ + 4] = f(g_cq).reshape(4, 128).T
    gcols[:, G_CKV:G_CKV + 2] = f(g_ckv).reshape(2, 128).T
    gcols[:, G_EPS] = EPS
    gcols[0:64, G_MLO] = 1.0
    gcols[64:128, G_MHI] = 1.0
    shared = {
        "rmat": _rmats(), "gcols": gcols,
        "w_in": f(w_in)[0], "w_q_b": f(w_q_b)[0], "w_kv_b": f(w_kv_b)[0], "w_mem_kv": f(w_mem_kv)[0],
        "w_br_a": f(w_br_a)[0], "w_br_b": f(w_br_b)[0], "w_br_m": f(w_br_m)[0], "w_out": f(w_out)[0],
        "w_up": f(w_up)[0], "w_down": f(w_down)[0],
    }
    tp = np.arange(LP)
    ts = np.arange(LS)
    in_maps = []
    xpT = np.ascontiguousarray(x_prompt[0].T)
    for c in cores:
        b, hf = c // 2, c % 2
        xsT = x_sample[b].T
        qp = np.arange(1024 * c, 1024 * (c + 1))
        qs = np.arange(1024 * hf, 1024 * (hf + 1))
        m = dict(shared)
        m["xq"] = np.ascontiguousarray(np.concatenate([xpT[:, qp], xsT[:, qs]], axis=1))
        m["xc"] = np.ascontiguousarray(np.concatenate([xpT, xsT], axis=1))
        m["memT"] = np.ascontiguousarray(np.concatenate([mem_prompt[0].T, mem_sample[b].T], axis=1))
        crow = np.concatenate([tp // 64, ts // 64])
        ccol = np.concatenate([tp % 64, ts % 64])
        qrow = np.concatenate([qp // 64, qs // 64])
        qcol = np.concatenate([qp % 64, qs % 64])
        m["tabA_c"] = _rope_tables(crow, ccol, "A")
        m["tabB_c"] = _rope_tables(crow, ccol, "B")
        m["tabA_q"] = _rope_tables(qrow, qcol, "A")
        m["tabB_q"] = _rope_tables(qrow, qcol, "B")
        in_maps.append(m)
    return in_maps


def kernel(**inputs):
    in_maps = prep_inputs(**inputs)
    if "nc" not in _NC_CACHE:
        _NC_CACHE["nc"] = build_program()
    nc = _NC_CACHE["nc"]
    res = run_bass_kernel_spmd(nc, in_maps, core_ids=list(range(NCORE)))
    y_prompt = np.empty((1, LP, D), np.float32)
    y_sample = np.empty((4, LS, D), np.float32)
    for c in range(NCORE):
        b, hf = c // 2, c % 2
        yt = res.results[c]["yT"]
        y_prompt[0, 1024 * c:1024 * (c + 1), :] = yt[:, 0:1024].T
        y_sample[b, 1024 * hf:1024 * (hf + 1), :] = yt[:, 1024:2048].T
    return (y_prompt, y_sample)
```

```python
import numpy as np
import concourse.bass as bass
import concourse.mybir as mybir
from concourse.bass_utils import run_bass_kernel_spmd

F32 = mybir.dt.float32
BF16 = mybir.dt.bfloat16
ALU = mybir.AluOpType
AF = mybir.ActivationFunctionType

ENGS = ("pe", "act", "dve", "pool", "sp")
D = 2048
NCORE = 8
LP, LS = 8192, 2048
LC = LP + LS
TQ = 512
NJOB = 4
EPS = 1e-6
O_QA, O_KA, O_VA, O_CQ, O_CKV, O_KR, O_QM, O_G = 0, 1024, 1280, 1536, 2048, 2304, 2368, 2880
G_MIX, G_MLP, G_FIN, G_MEM, G_QA, G_KA, G_CQ, G_CKV, G_EPS, G_MLO, G_MHI, NG = 0, 16, 32, 48, 64, 65, 66, 70, 72, 73, 74, 75
SLOTW = 6144
DEBUG = False
KNOB = {"ntail": 14, "n1": 41, "lag": 4, "pool_den": True, "s5": 9, "s1": 99, "nctx": LC // 512, "mem": True, "njob": NJOB, "phase": "g"}


class Buf:
    def __init__(self, name, ap=None, accum=False):
        self.name = name
        self.ap = ap
        self.w = {}
        self.r = {}
        self.accum = accum
        self.dsem = None
        self.psum = False
        self.aliases = []


class Tracker:
    def __init__(self, nc):
        self.nc = nc
        self.q = {e: [] for e in ENGS}
        self.cnt = {e: 0 for e in ENGS}
        self.waited = {e: {} for e in ENGS}
        self.dtotal = {}
        self.sems = {}
        self.ndsem = 0
        self.self_wait = True

    def sem(self, name):
        if name not in self.sems:
            self.sems[name] = self.nc.alloc_semaphore(name)
        return self.sems[name]

    def _dsem_for(self, buf, queue):
        kind = "sw" if queue == "pool" else "hw"
        if buf.dsem is None:
            buf.dsem = {}
        if kind not in buf.dsem:
            name = "d%d" % self.ndsem
            self.ndsem += 1
            self.dtotal[name] = 0
            self.sem(name)
            buf.dsem[kind] = name
        return buf.dsem[kind]

    def _wait(self, eng, toks):
        for s, v in toks:
            if s in self.dtotal:
                pass
            else:
                if s == eng and (eng == "pe" or not self.self_wait):
                    continue
                assert v <= self.cnt[s], (
                    "wait on unissued milestone %s %d > %d (eng %s)" % (s, v, self.cnt[s], eng))
            if v <= 0 or self.waited[eng].get(s, 0) >= v:
                continue
            self.waited[eng][s] = v
            self.q[eng].append(("wait", s, v))

    def _deps(self, reads, writes, eng=None):
        toks = []
        for b in reads:
            toks.extend(b.w.items())
            if b.psum:
                toks.extend(b.r.items())
        for b in writes:
            toks.extend(b.w.items())
            toks.extend(b.r.items())
        return toks

    @staticmethod
    def _expand(bufs):
        return [x for b in bufs for x in [b] + b.aliases]

    def op(self, eng, fn, reads=(), writes=(), inc=True):
        reads, writes = self._expand(reads), self._expand(writes)
        self._wait(eng, self._deps(reads, writes, eng))
        if inc:
            self.cnt[eng] += 1
            v = self.cnt[eng]
        else:
            v = self.cnt[eng] + 1
        self.q[eng].append(("op", fn, eng if inc else None, 1))
        for b in reads:
            b.r[eng] = max(b.r.get(eng, 0), v)
        for b in writes:
            if b.accum:
                b.w[eng] = max(b.w.get(eng, 0), v)
            else:
                b.w = {eng: v}
                b.r = {}

    def dma(self, queue, out_ap, in_ap, reads=(), writes=(), sem_buf=None):
        reads, writes = self._expand(reads), self._expand(writes)
        self._wait(queue, self._deps(reads, writes, queue))
        s = self._dsem_for(sem_buf, queue)
        self.dtotal[s] += 16
        v = self.dtotal[s]
        self.q[queue].append(("op", (lambda e, o=out_ap, i=in_ap: e.dma_start(out=o, in_=i)), s, 16))
        for b in reads:
            b.r[s] = v
        for b in writes:
            if b.accum:
                b.w[s] = v
            else:
                b.w = {s: v}
                b.r = {}

    def final_wait(self, eng="sp"):
        toks = [(e, self.cnt[e]) for e in ("pe", "act", "dve", "pool")]
        toks += [(s, v) for s, v in self.dtotal.items()]
        self._wait(eng, [t for t in toks if t[0] != eng])

    def emit(self):
        nc = self.nc
        for e in ("pe", "act", "dve", "pool"):
            self.sem(e)
        engmap = {"pe": "tensor", "act": "scalar", "dve": "vector", "pool": "gpsimd", "sp": "sync"}
        with nc.Block() as block:
            for e in ENGS:
                ops = self.q[e]

                def body(h, ops=ops):
                    for o in ops:
                        if o[0] == "wait":
                            h.wait_ge(self.sems[o[1]], o[2])
                        else:
                            ins = o[1](h)
                            if o[2] is not None:
                                ins.then_inc(self.sems[o[2]], o[3])

                getattr(block, engmap[e])(body)


class Ring:
    def __init__(self, bufs):
        self.bufs = bufs
        self.i = 0

    def next(self):
        b = self.bufs[self.i % len(self.bufs)]
        self.i += 1
        return b


def build_program():
    nc = bass.Bass("TRN2", target_bir_lowering=False)
    T = Tracker(nc)

    def din(name, shape):
        return nc.dram_tensor(name, list(shape), F32, kind="ExternalInput").ap()

    xq = din("xq", [D, NJOB * TQ])
    xc = din("xc", [D, LC])
    memT = din("memT", [D, 512])
    tabA_c = din("tabA_c", [128, 2, LC])
    tabB_c = din("tabB_c", [128, 2, LC])
    tabA_q = din("tabA_q", [128, 2, NJOB * TQ])
    tabB_q = din("tabB_q", [128, 2, NJOB * TQ])
    rmat = din("rmat", [128, 256])
    gcols_d = din("gcols", [128, NG])
    w_in = din("w_in", [D, 9024])
    w_q_b = din("w_q_b", [512, 1536])
    w_kv_b = din("w_kv_b", [256, 2048])
    w_mem_kv = din("w_mem_kv", [D, 1024])
    w_br_a = din("w_br_a", [1024, D])
    w_br_b = din("w_br_b", [1024, D])
    w_br_m = din("w_br_m", [512, D])
    w_out = din("w_out", [D, D])
    w_up = din("w_up", [D, 4 * D])
    w_down = din("w_down", [4 * D, D])
    yT = nc.dram_tensor("yT", [D, NJOB * TQ], F32, kind="ExternalOutput").ap()

    okind = "ExternalOutput" if DEBUG else "Internal"
    KA = nc.dram_tensor("KA", [2, 128, LC], BF16, kind=okind).ap()
    VA = nc.dram_tensor("VA", [2, 128, LC // 128, 128], BF16, kind=okind).ap()
    KB = nc.dram_tensor("KB", [8, 128, LC], BF16, kind=okind).ap()
    KPE = nc.dram_tensor("KPE", [2, 128, LC], BF16, kind=okind).ap()
    VB = nc.dram_tensor("VB", [8, 128, LC // 128, 128], BF16, kind=okind).ap()
    KA_b, VA_b, KB_b, KPE_b, VB_b = (Buf(n, accum=True) for n in ("KA", "VA", "KB", "KPE", "VB"))

    NWORDS = 53000
    arena = nc.alloc_sbuf_tensor("arena", [128, NWORDS], F32)
    apos = [0]

    def alloc_f(name, n):
        o = apos[0]
        apos[0] += n
        assert apos[0] <= NWORDS, "SBUF arena overflow at %s: %d" % (name, apos[0])
        b = Buf(name, arena[:, o:o + n])
        b.off = o
        return b

    def alloc_b(name, n):
        assert n % 2 == 0
        o = apos[0]
        apos[0] += n // 2
        assert apos[0] <= NWORDS, "SBUF arena overflow at %s: %d" % (name, apos[0])
        b = Buf(name, arena[:, o:o + n // 2].bitcast(BF16))
        b.off = o
        return b

    ps_t = [nc.alloc_psum_tensor("psb%d" % i, [128, 512], F32) for i in range(8)]
    PSB = [Buf("ps%d" % i, ps_t[i][:, :]) for i in range(8)]
    for b_ in PSB:
        b_.psum = True
    ps_acc = PSB[0:2]
    ps_ring = Ring(PSB[2:8])

    ones_b = alloc_b("ones_b", 128)
    ones_f = alloc_f("ones_f", 128)
    rm_b = alloc_b("rm_b", 256)
    gc = alloc_f("gcols", NG + 3)
    KM = alloc_b("KM", 2 * 4 * 256)
    VM = alloc_b("VM", 2 * 2 * 512)
    wring = Ring([alloc_b("wslot%d" % i, SLOTW) for i in range(3)])
    xring = Ring([alloc_f("xslot%d" % i, 4 * 512) for i in range(2)])
    sqring = Ring([alloc_b("sq%d" % i, 512) for i in range(4)])
    tmpf = Ring([alloc_f("tmpf%d" % i, 512) for i in range(5)])
    tmpb = Ring([alloc_b("tmpb%d" % i, 512) for i in range(3)])
    rstd_x = alloc_f("rstd_x", 512)
    rstd_x2 = alloc_f("rstd_x2", 512)
    rstd_t = alloc_f("rstd_t", 8)
    tabA = alloc_f("tabA", 1024)
    tabB = alloc_f("tabB", 1024)
    xg = alloc_b("xg", 16 * 512)
    base_pos = apos[0]

    def gcol(i):
        return gc.ap[:, i:i + 1]

    epsc = gc.ap[:, G_EPS:G_EPS + 1]

    def MM(psb, out_ap, lhsT, rhs, start, stop, reads, last):
        T.op("pe", lambda e: e.matmul(out_ap, lhsT, rhs, start=start, stop=stop),
             reads=reads, writes=[psb], inc=last)

    def ACT(out_ap, in_ap, func, reads, writes, **kw):
        T.op("act", lambda e: e.activation(out=out_ap, in_=in_ap, func=func, **kw), reads=reads, writes=writes)

    def TT(out_ap, in0, in1, op, reads, writes, eng="dve"):
        T.op(eng, lambda e: e.tensor_tensor(out=out_ap, in0=in0, in1=in1, op=op), reads=reads, writes=writes)

    def STT(out_ap, in0, scalar, in1, op0, op1, reads, writes):
        T.op("dve", lambda e: e.scalar_tensor_tensor(out=out_ap, in0=in0, scalar=scalar, in1=in1, op0=op0, op1=op1),
             reads=reads, writes=writes)

    def TS(out_ap, in0, s1, op0, reads, writes, s2=None, op1=None):
        if op1 is None:
            T.op("dve", lambda e: e.tensor_scalar(out=out_ap, in0=in0, scalar1=s1, scalar2=None, op0=op0),
                 reads=reads, writes=writes)
        else:
            T.op("dve", lambda e: e.tensor_scalar(out=out_ap, in0=in0, scalar1=s1, scalar2=s2, op0=op0, op1=op1),
                 reads=reads, writes=writes)

    def rsq(outb, out_ap, in_ap, in_bufs):
        ACT(out_ap, in_ap, AF.Ln, reads=in_bufs + [gc], writes=[outb], bias=epsc, scale=1.0)
        ACT(out_ap, out_ap, AF.Exp, reads=[outb], writes=[outb], scale=-0.5)

    def cp_evac(k, out_ap, in_ap, reads, writes):
        if k % 2 == 0:
            ACT(out_ap, in_ap, AF.Copy, reads=reads, writes=writes)
        else:
            T.op("dve", lambda e: e.tensor_copy(out=out_ap, in_=in_ap), reads=reads, writes=writes)

    T.op("dve", lambda e: e.memset(ones_b.ap, 1.0), writes=[ones_b])
    T.op("dve", lambda e: e.memset(ones_f.ap, 1.0), writes=[ones_f])
    T.dma("pool", rm_b.ap, rmat, writes=[rm_b], sem_buf=rm_b)
    T.dma("sp", gc.ap[:, 0:NG], gcols_d, writes=[gc], sem_buf=gc)
    RA = rm_b.ap[:, 0:128]
    RB = rm_b.ap[:, 128:256]

    def rope(xb_buf, xb_ap, tab_buf, c_ap, s_ap, R, out_ap, out_bufs, n=512):
        pr = ps_ring.next()
        MM(pr, pr.ap[:, 0:n], R, xb_ap, True, True, [rm_b, xb_buf], True)
        t1 = tmpf.next()
        TT(t1.ap[:, 0:n], xb_ap, c_ap, ALU.mult, [xb_buf, tab_buf], [t1])
        t2 = tmpf.next()
        TT(t2.ap[:, 0:n], pr.ap[:, 0:n], s_ap, ALU.mult, [pr, tab_buf], [t2])
        TT(out_ap, t1.ap[:, 0:n], t2.ap[:, 0:n], ALU.add, [t1, t2], out_bufs)

    cur = {}

    def headnorm_part1(ps, gidx):
        rx, rx2 = cur["rstd_x"], cur["rstd_x2"]
        sq = sqring.next()
        ACT(sq.ap, ps.ap, AF.Square, [ps], [sq], scale=float(128 ** -0.5))
        pss = ps_ring.next()
        MM(pss, pss.ap, ones_b.ap, sq.ap, True, True, [ones_b, sq], True)
        u = tmpf.next()
        TT(u.ap, pss.ap, rx2.ap, ALU.mult, [pss, rx2], [u])
        rsq(u, u.ap, u.ap, [u])
        TT(u.ap, u.ap, rx.ap, ALU.mult, [u, rx], [u])
        qn = tmpb.next()
        STT(qn.ap, ps.ap, gcol(gidx), u.ap, ALU.mult, ALU.mult, [ps, gc, u], [qn])
        return qn

    def load_x_group(src, col0, ncol, g):
        xs = cur.get("xring", xring).next()
        v = xs.ap[:, 0:4 * ncol].rearrange("p (a n) -> p a n", n=ncol)
        T.dma("sp", v, src[g * 512:(g + 1) * 512, col0:col0 + ncol].rearrange("(a p) n -> p a n", p=128),
              writes=[xs], sem_buf=xs)
        return xs, v

    def build_xg(src, col0, ncol, gbase):
        xgb, rx, rx2 = cur["xg"], cur["rstd_x"], cur["rstd_x2"]
        pss = ps_ring.next()
        for g in range(4):
            xs, v = load_x_group(src, col0, ncol, g)
            for j in range(4):
                kc = 4 * g + j
                sq = sqring.next()
                ACT(sq.ap[:, 0:ncol], v[:, j, :], AF.Square, [xs], [sq], scale=float(D ** -0.5))
                MM(pss, pss.ap[:, 0:ncol], ones_b.ap, sq.ap[:, 0:ncol], kc == 0, kc == 15, [ones_b, sq], True)
                TS(xgb.ap[:, kc * 512:kc * 512 + ncol], v[:, j, :], gcol(gbase + kc), ALU.mult, [xs, gc], [xgb])
        rsq(rx, rx.ap[:, 0:ncol], pss.ap[:, 0:ncol], [pss])
        TT(rx2.ap[:, 0:ncol], rx.ap[:, 0:ncol], rx.ap[:, 0:ncol], ALU.mult, [rx], [rx2])

    def tok_rstd(ntt):
        rx, rt = cur["rstd_x"], cur["rstd_t"]
        pt = ps_ring.next()
        for tt in range(ntt):
            MM(pt, pt.ap[:, tt:tt + 1], rx.ap[0:1, tt * 128:(tt + 1) * 128], ones_f.ap[0:1, 0:1],
               True, True, [rx, ones_f], tt == ntt - 1)
        T.op("dve", lambda e: e.tensor_copy(out=rt.ap[:, 0:ntt], in_=pt.ap[:, 0:ntt]), reads=[pt], writes=[rt])

    m1 = apos[0]
    wkv = alloc_b("wkv", 16 * 896)
    wkvb = alloc_b("wkvb", 2 * 2048)
    craw = alloc_f("craw", 2 * 512)
    ckvn = alloc_b("ckvn", 2 * 512)
    ka_st = alloc_b("ka_st", 2 * 512)
    va_st = alloc_b("va_st", 2 * 4 * 128)
    kb_st = alloc_b("kb_st", 8 * 512)
    vb_st = alloc_b("vb_st", 8 * 4 * 128)
    kpe_st = alloc_b("kpe_st", 512)
    kpe2_st = alloc_b("kpe2_st", 2 * 512)
    xg2 = alloc_b("xg2", 16 * 512)
    rstd_xb = alloc_f("rstd_xb", 512)
    rstd_x2b = alloc_f("rstd_x2b", 512)
    rstd_tb = alloc_f("rstd_tb", 8)
    nxs = min(4, (NWORDS - apos[0]) // 2048)
    xring_s1 = Ring(xring.bufs + [alloc_f("xslot_s1_%d" % i, 4 * 512) for i in range(nxs)])
    CUR = [dict(xg=xg, rstd_x=rstd_x, rstd_x2=rstd_x2, rstd_t=rstd_t),
           dict(xg=xg2, rstd_x=rstd_xb, rstd_x2=rstd_x2b, rstd_t=rstd_tb)]
    cur.update(CUR[0])

    wkv_v = wkv.ap.rearrange("p (k n) -> p k n", n=896)
    for (dst0, src0, n) in ((0, O_KA, 256), (256, O_VA, 256), (512, O_CKV, 256), (768, O_KR, 64), (832, O_KR, 64)):
        T.dma("pool", wkv_v[:, :, dst0:dst0 + n], w_in[:, src0:src0 + n].rearrange("(k p) n -> p k n", p=128),
              writes=[wkv], sem_buf=wkv)
    wkvb_v = wkvb.ap.rearrange("p (k n) -> p k n", n=2048)
    T.dma("pool", wkvb_v, w_kv_b.rearrange("(k p) n -> p k n", p=128), writes=[wkvb], sem_buf=wkvb)
    wkvb_h = wkvb.ap.rearrange("p (k h n) -> p k h n", k=2, h=8)
    xg_v = xg.ap.rearrange("p (k n) -> p k n", n=512)

    def fence():
        wsems = set(n for b in list(wring.bufs) + [cvX, cvG] for n in (b.dsem or {}).values())
        toks = [(e, T.cnt[e]) for e in ("pe", "act", "dve", "pool")] + [(k, v) for k, v in T.dtotal.items() if k not in wsems]
        for e in ENGS:
            T._wait(e, [t for t in toks if t[0] != e])

    s1_end = apos[0]
    apos[0] = m1
    oT = alloc_b("oT", 20 * 512)
    gates = [alloc_f("gate%d" % i, 512) for i in range(3)]
    rstd2 = alloc_f("rstd2", 512)
    yring = Ring([alloc_f("yst%d" % i, 512) for i in range(2)])
    z0 = apos[0]
    hbuf = alloc_f("hbuf", 16 * 512)
    hg = alloc_b("hg", 16 * 512)
    merged = alloc_b("merged", 16 * 512)
    z1 = apos[0]
    apos[0] = z1 - 4096
    aq = alloc_b("aq", 16 * 512)
    apos[0] = z0
    qa = alloc_b("qa", 8 * 512)
    qpe = alloc_b("qpe", 4 * 512)
    cqn = alloc_b("cqn", 4 * 512)
    cqraw = alloc_f("cqraw", 4 * 512)
    NACC = 4
    daccs = [alloc_f("dacc%d" % i, 512) for i in range(NACC)]
    rden = alloc_f("rden", 512)
    kvring = Ring([(alloc_b("kc%d" % i, 1024), alloc_b("pc%d" % i, 1024), alloc_b("vc%d" % i, 1024)) for i in range(3)])
    pring = Ring([alloc_b("pT%d" % i, 512) for i in range(6)])
    assert apos[0] <= z1, (apos[0], z1)
    apos[0] = z1
    qa_v = qa.ap.rearrange("p (h n) -> p h n", n=512)
    qpe_v = qpe.ap.rearrange("p (h n) -> p h n", n=512)
    cqn_v = cqn.ap.rearrange("p (h n) -> p h n", n=512)
    oT_v = oT.ap.rearrange("p (h n) -> p h n", n=512)
    mg_v = merged.ap.rearrange("p (h n) -> p h n", n=512)
    hb_v = hbuf.ap.rearrange("p (h n) -> p h n", n=512)
    hg_v = hg.ap.rearrange("p (h n) -> p h n", n=512)
    aq_v = aq.ap.rearrange("p (h n) -> p h n", n=512)

    tasks = []

    jl_cnt = [0]
    cur_idx = [0]
    WMODE = ["convert"]
    nused_tab = {}

    def wtask(load, compute):
        if load is not None:
            idx = jl_cnt[0]
            jl_cnt[0] += 1

            def load2(ws, idx=idx, load=load):
                cur_idx[0] = idx
                load(ws)
            tasks.append((load2, compute))
        else:
            tasks.append((None, compute))

    def run_tasks(depth=2):
        n = len(tasks)
        slots = [None] * n
        lidx = [i for i in range(n) if tasks[i][0] is not None]
        nxt = 0
        for i in range(n):
            while nxt < len(lidx) and (lidx[nxt] <= i or sum(1 for j in lidx[:nxt] if j > i) < depth):
                k = lidx[nxt]
                slots[k] = wring.next()
                tasks[k][0](slots[k])
                nxt += 1
            tasks[i][1](slots[i])
        tasks.clear()

    def wload(ws, views, nused):
        if WMODE[0] == "convert":
            for d, s in views:
                T.dma("pool", d, s, writes=[ws], sem_buf=ws)
            nused_tab[cur_idx[0]] = nused
        else:
            i = cur_idx[0]
            T.dma("sp", ws.ap[:, 0:nused], Wc[i, :, 0:nused], reads=[Wc_bufs[i]], writes=[ws], sem_buf=ws)

    def proj16(wsrc, col0, ntile, consume, rhs_v, rhs_buf):
        i = 0
        pend_def = []
        while i < ntile:
            n = min(3, ntile - i)

            def load(ws, i=i, n=n):
                wv = ws.ap[:, 0:16 * n * 128].rearrange("p (k n) -> p k n", n=n * 128)
                wload(ws, [(wv, wsrc[:, col0 + i * 128:col0 + (i + n) * 128].rearrange("(k p) n -> p k n", p=128))], 16 * n * 128)

            def comp(ws, i=i, n=n):
                wv = ws.ap[:, 0:16 * n * 128].rearrange("p (k n) -> p k n", n=n * 128)
                for jl in range(n):
                    ps = ps_ring.next()
                    for kc in range(16):
                        MM(ps, ps.ap, wv[:, kc, jl * 128:(jl + 1) * 128], rhs_v[:, kc, :], kc == 0, kc == 15,
                           [ws, rhs_buf], kc == 15)
                    t = i + jl
                    due = sorted([e for e in pend_def if e[0] <= t], key=lambda e: e[0])
                    for e in due:
                        pend_def.remove(e)
                        e[1]()
                    d = consume(t, ps)
                    if d is not None:
                        if callable(d):
                            d = [(1, d)]
                        for delay, fn in d:
                            pend_def.append((t + delay, fn))
                if i + n >= ntile:
                    for e in sorted(pend_def, key=lambda e: e[0]):
                        e[1]()
                    del pend_def[:]

            wtask(load, comp)
            i += n

    def attention(kind, jb, c0, L, mj):
        nh = {"A": 8, "B": 8, "M": 4}[kind]
        obase = {"A": 0, "B": 8, "M": 16}[kind]
        scale = {"A": 128 ** -0.5, "B": 192 ** -0.5, "M": 128 ** -0.5}[kind]
        nch = 1 if kind == "M" else L // 1024
        steps = [(h, ch) for h in range(nh) for ch in range(nch)]
        loaded = {}

        def issue(si):
            h, ch = steps[si]
            kc_, pc_, vc_ = kvring.next()
            t0 = c0 + ch * 1024
            if kind == "A":
                T.dma("sp", kc_.ap, KA[h // 4, :, t0:t0 + 1024], reads=[KA_b], writes=[kc_], sem_buf=kc_)
                T.dma("sp", vc_.ap.rearrange("p (t d) -> p t d", d=128), VA[h // 4, :, t0 // 128:t0 // 128 + 8, :],
                      reads=[VA_b], writes=[vc_], sem_buf=vc_)
            else:
                T.dma("sp", kc_.ap, KB[h, :, t0:t0 + 1024], reads=[KB_b], writes=[kc_], sem_buf=kc_)
                T.dma("sp", pc_.ap, KPE[h % 2, :, t0:t0 + 1024], reads=[KPE_b], writes=[pc_], sem_buf=pc_)
                T.dma("sp", vc_.ap.rearrange("p (t d) -> p t d", d=128), VB[h, :, t0 // 128:t0 // 128 + 8, :],
                      reads=[VB_b], writes=[vc_], sem_buf=vc_)
            loaded[si] = (kc_, pc_, vc_)

        nissued = 0
        pend = []
        nd = [0]
        for si, (h, ch) in enumerate(steps):
            if kind != "M":
                while nissued < min(len(steps), si + 2):
                    issue(nissued)
                    nissued += 1
                kc_, pc_, vc_ = loaded.pop(si)
            nkt = 2 if kind == "M" else 8
            pso = ps_acc[0]
            for kt in range(nkt):
                first = (ch == 0 and kt == 0)
                last = (ch == nch - 1 and kt == nkt - 1)
                pss = ps_ring.next()
                if kind == "A":
                    MM(pss, pss.ap, kc_.ap[:, kt * 128:(kt + 1) * 128], qa_v[:, h, :], True, True, [kc_, qa], True)
                elif kind == "B":
                    MM(pss, pss.ap, kc_.ap[:, kt * 128:(kt + 1) * 128], qa_v[:, h, :], True, False, [kc_, qa], False)
                    MM(pss, pss.ap, pc_.ap[:, kt * 128:(kt + 1) * 128], qpe_v[:, h // 2, :], False, True,
                       [pc_, qpe], True)
                else:
                    MM(pss, pss.ap, KM_v[:, mj, h, kt * 128:(kt + 1) * 128], qa_v[:, h, :], True, True, [KM, qa], True)
                pT = pring.next()
                ACT(pT.ap, pss.ap, AF.Exp, [pss], [pT], scale=float(scale))
                ti = ch * nkt + kt
                pe_every = {"A": 3, "B": 4, "M": 0}[kind]
                if pe_every and ti % pe_every == 0:
                    dmode = "pe"
                else:
                    dmode = "dve"
                    a = daccs[nd[0] % NACC]
                    if nd[0] < NACC:
                        T.op("dve", lambda e, o=a.ap, i_=pT.ap: e.tensor_copy(out=o, in_=i_), reads=[pT], writes=[a])
                    else:
                        TT(a.ap, a.ap, pT.ap, ALU.add, [a, pT], [a])
                    nd[0] += 1
                if kind == "M":
                    vl, vb_ = VM_v[:, mj, kt, h * 128:(h + 1) * 128], VM
                else:
                    vl, vb_ = vc_.ap[:, kt * 128:(kt + 1) * 128], vc_
                pend.append((vl, vb_, pT, first, last, dmode == "pe"))

                def do_pv(ent):
                    vl2, vb2, pT2, f2, l2, pe_den = ent
                    MM(pso, pso.ap, vl2, pT2.ap, f2, l2, [vb2, pT2], l2)
                    if pe_den:
                        MM(ps_acc[1], ps_acc[1].ap, ones_b.ap, pT2.ap, f2, False, [ones_b, pT2], False)
                if len(pend) > KNOB["lag"]:
                    do_pv(pend.pop(0))
                if last:
                    while pend:
                        do_pv(pend.pop(0))
                    used = min(nd[0], NACC)
                    if pe_every:
                        psd = ps_acc[1]
                        for k in range(used):
                            MM(psd, psd.ap, ones_f.ap, daccs[k].ap, False, k == used - 1, [ones_f, daccs[k]], k == used - 1)
                    else:
                        psd = ps_ring.next()
                        for k in range(used):
                            MM(psd, psd.ap, ones_f.ap, daccs[k].ap, k == 0, k == used - 1, [ones_f, daccs[k]], k == used - 1)
                    nd[0] = 0
                    T.op("dve", lambda e, o=rden.ap, i_=psd.ap: e.reciprocal(out=o, in_=i_), reads=[psd], writes=[rden])
                    TT(oT_v[:, obase + h, :], pso.ap, rden.ap, ALU.mult, [pso, rden], [oT])

    conv2 = []

    def make_job(jb):
        jl_cnt[0] = 0
        q0 = jb * TQ
        PH = KNOB["phase"]
        is_p = jb < 2
        c0, L, mj = (0, LP, 0) if is_p else (LP, LS, 1)

        def t_a(ws, q0=q0):
            fence()
            T.dma("sp", tabA.ap.rearrange("p (a n) -> p a n", n=512), tabA_q[:, :, q0:q0 + 512], writes=[tabA], sem_buf=tabA)
            T.dma("sp", tabB.ap.rearrange("p (a n) -> p a n", n=512), tabB_q[:, :, q0:q0 + 512], writes=[tabB], sem_buf=tabB)
            build_xg(xq, q0, 512, G_MIX)
        wtask(None, t_a)
        if jb == 0 and conv2:
            wtask(None, lambda ws: conv2.pop()())

        def c_qa(i, ps):
            rx, rx2 = cur["rstd_x"], cur["rstd_x2"]
            sq = sqring.next()
            ACT(sq.ap, ps.ap, AF.Square, [ps], [sq], scale=float(128 ** -0.5))
            raw = cqraw.ap[:, (i % 4) * 512:(i % 4 + 1) * 512]
            ACT(raw, ps.ap, AF.Copy, [ps], [cqraw])
            qn_ap = cqn_v[:, i % 4, :]

            def stage1():
                pss = ps_ring.next()
                MM(pss, pss.ap, ones_b.ap, sq.ap, True, True, [ones_b, sq], True)
                u = tmpf.next()
                TT(u.ap, pss.ap, rx2.ap, ALU.mult, [pss, rx2], [u])
                rsq(u, u.ap, u.ap, [u])
                TT(u.ap, u.ap, rx.ap, ALU.mult, [u, rx], [u])
                STT(qn_ap, raw, gcol(G_QA), u.ap, ALU.mult, ALU.mult, [cqraw, gc, u], [cqn])

            def stage2():
                rope(cqn, qn_ap, tabA, tabA.ap[:, 0:512], tabA.ap[:, 512:1024], RA, qa_v[:, i, :], [qa])
            return [(1, stage1), (3, stage2)]
        proj16(w_in, O_QA, 8, c_qa, xg_v, xg)
        wtask(None, lambda ws, jb=jb, c0=c0, L=L, mj=mj: attention("A", jb, c0, L, mj))

        cq_sq = []

        def c_cq(i, ps, cq_sq=cq_sq):
            sq = alloc_sq[i]
            ACT(sq.ap, ps.ap, AF.Square, [ps], [sq], scale=float(512 ** -0.5))
            ACT(cqraw.ap[:, i * 512:(i + 1) * 512], ps.ap, AF.Copy, [ps], [cqraw])
            if i == 3:
                pss = ps_ring.next()
                for k in range(4):
                    MM(pss, pss.ap, ones_b.ap, alloc_sq[k].ap, k == 0, k == 3, [ones_b, alloc_sq[k]], k == 3)
                u = tmpf.next()
                TT(u.ap, pss.ap, rstd_x2.ap, ALU.mult, [pss, rstd_x2], [u])
                rsq(u, u.ap, u.ap, [u])
                TT(u.ap, u.ap, rstd_x.ap, ALU.mult, [u, rstd_x], [u])
                for k in range(4):
                    STT(cqn_v[:, k, :], cqraw.ap[:, k * 512:(k + 1) * 512], gcol(G_CQ + k), u.ap, ALU.mult, ALU.mult,
                        [cqraw, gc, u], [cqn])
        alloc_sq = sqring.bufs
        proj16(w_in, O_CQ, 4, c_cq, xg_v, xg)

        for half in range(2):
            def load(ws, half=half):
                wv = ws.ap[:, 0:4 * 768].rearrange("p (k n) -> p k n", n=768)
                wpe = ws.ap[:, 3072:4096].rearrange("p (k j n) -> p k j n", k=4, j=2)
                views = [(wv, w_q_b[:, half * 768:(half + 1) * 768].rearrange("(k p) n -> p k n", p=128))]
                for hl in range(4):
                    h = half * 4 + hl
                    views.append((wpe[:, :, hl // 2, (hl % 2) * 64:(hl % 2) * 64 + 64],
                                  w_q_b[:, h * 192 + 128:h * 192 + 192].rearrange("(k p) n -> p k n", p=128)))
                wload(ws, views, 4096)

            def comp(ws, half=half):
                wv = ws.ap[:, 0:4 * 768].rearrange("p (k h n) -> p k h n", k=4, h=4)
                wpe = ws.ap[:, 3072:4096].rearrange("p (k j n) -> p k j n", k=4, j=2)
                for hl in range(4):
                    h = half * 4 + hl
                    ps = ps_ring.next()
                    for kc in range(4):
                        MM(ps, ps.ap, wv[:, kc, hl, 0:128], cqn_v[:, kc, :], kc == 0, kc == 3, [ws, cqn], kc == 3)
                    cp_evac(hl, qa_v[:, h, :], ps.ap, [ps], [qa])
                for jl in range(2):
                    ps = ps_ring.next()
                    for kc in range(4):
                        MM(ps, ps.ap, wpe[:, kc, jl, :], cqn_v[:, kc, :], kc == 0, kc == 3, [ws, cqn], kc == 3)
                    qr = tmpb.next()
                    cp_evac(jl, qr.ap, ps.ap, [ps], [qr])
                    rope(qr, qr.ap, tabB, tabB.ap[:, 0:512], tabB.ap[:, 512:1024], RB, qpe_v[:, half * 2 + jl, :], [qpe])
            wtask(load, comp)
        wtask(None, lambda ws, jb=jb, c0=c0, L=L, mj=mj: attention("B", jb, c0, L, mj))

        def c_qm(i, ps):
            TT(qa_v[:, i, :], ps.ap, rstd_x.ap, ALU.mult, [ps, rstd_x], [qa])
        proj16(w_in, O_QM, 4, c_qm, xg_v, xg)
        wtask(None, lambda ws, jb=jb, c0=c0, L=L, mj=mj: attention("M", jb, c0, L, mj))

        wtask(None, lambda ws: fence())
        for j in range(16):
            def loadg(ws, j=j):
                wv = ws.ap[:, 0:16 * 384].rearrange("p (k b n) -> p k b n", k=16, b=3)
                wload(ws, [(wv[:, :, b, :],
                            w_in[:, O_G + b * D + j * 128:O_G + b * D + (j + 1) * 128].rearrange("(k p) n -> p k n", p=128))
                           for b in range(3)], 16 * 384)

            def compg(ws, j=j):
                wv = ws.ap[:, 0:16 * 384].rearrange("p (k b n) -> p k b n", k=16, b=3)
                for b in range(3):
                    ps = ps_ring.next()
                    for kc in range(16):
                        MM(ps, ps.ap, wv[:, kc, b, :], xg_v[:, kc, :], kc == 0, kc == 15, [ws, xg], kc == 15)
                    t = tmpf.next()
                    TT(t.ap, ps.ap, rstd_x.ap, ALU.mult, [ps, rstd_x], [t])
                    ACT(gates[b].ap, t.ap, AF.Sigmoid, [t], [gates[b]])

            def loadb(ws, j=j):
                wv = ws.ap[:, 0:20 * 128].rearrange("p (k n) -> p k n", n=128)
                wload(ws, [(wv[:, 0:8, :], w_br_a[:, j * 128:(j + 1) * 128].rearrange("(k p) n -> p k n", p=128)),
                           (wv[:, 8:16, :], w_br_b[:, j * 128:(j + 1) * 128].rearrange("(k p) n -> p k n", p=128)),
                           (wv[:, 16:20, :], w_br_m[:, j * 128:(j + 1) * 128].rearrange("(k p) n -> p k n", p=128))], 20 * 128)

            def compb(ws, j=j):
                wv = ws.ap[:, 0:20 * 128].rearrange("p (k n) -> p k n", n=128)
                ms = []
                for b, (k0, nk) in enumerate(((0, 8), (8, 8), (16, 4))):
                    ps = ps_ring.next()
                    for kc in range(nk):
                        MM(ps, ps.ap, wv[:, k0 + kc, :], oT_v[:, k0 + kc, :], kc == 0, kc == nk - 1, [ws, oT], kc == nk - 1)
                    m = tmpf.next()
                    TT(m.ap, ps.ap, gates[b].ap, ALU.mult, [ps, gates[b]], [m])
                    ms.append(m)
                TT(ms[0].ap, ms[0].ap, ms[1].ap, ALU.add, [ms[0], ms[1]], [ms[0]])
                TT(mg_v[:, j, :], ms[0].ap, ms[2].ap, ALU.add, [ms[0], ms[2]], [merged])
            wtask(loadg, compg)
            wtask(loadb, compb)

        def c_wo(i, ps, q0=q0):
            xs = xring.next()
            T.dma("sp", xs.ap[:, 0:512], xq[i * 128:(i + 1) * 128, q0:q0 + 512], writes=[xs], sem_buf=xs)
            TT(hb_v[:, i, :], ps.ap, xs.ap[:, 0:512], ALU.add, [ps, xs], [hbuf])
            sq = sqring.next()
            ACT(sq.ap, hb_v[:, i, :], AF.Square, [hbuf], [sq], scale=float(D ** -0.5))
            MM(ps_acc[1], ps_acc[1].ap, ones_b.ap, sq.ap, i == 0, i == 15, [ones_b, sq], True)
            TS(hg_v[:, i, :], hb_v[:, i, :], gcol(G_MLP + i), ALU.mult, [hbuf, gc], [hg])
            if i == 15:
                rsq(rstd2, rstd2.ap, ps_acc[1].ap, [ps_acc[1]])
        proj16(w_out, 0, 16, c_wo, mg_v, merged)

        wtask(None, lambda ws: fence())
        for qd in range(4):
            def c_up(i, ps):
                t = tmpf.next()
                STT(t.ap, ps.ap, 0.0, rstd2.ap, ALU.max, ALU.mult, [ps, rstd2], [t])
                ACT(aq_v[:, i, :], t.ap, AF.Square, [t], [aq])
            proj16(w_up, qd * 2048, 16, c_up, hg_v, hg)

            def c_dn(i, ps):
                TT(hb_v[:, i, :], hb_v[:, i, :], ps.ap, ALU.add, [hbuf, ps], [hbuf])
            proj16(w_down[qd * 2048:(qd + 1) * 2048, :], 0, 16, c_dn, aq_v, aq)

        def t_g(ws, q0=q0):
            for i in range(16):
                sq = sqring.next()
                ACT(sq.ap, hb_v[:, i, :], AF.Square, [hbuf], [sq], scale=float(D ** -0.5))
                MM(ps_acc[1], ps_acc[1].ap, ones_b.ap, sq.ap, i == 0, i == 15, [ones_b, sq], True)
            rsq(rstd2, rstd2.ap, ps_acc[1].ap, [ps_acc[1]])
            for i in range(16):
                ys = yring.next()
                STT(ys.ap, hb_v[:, i, :], gcol(G_FIN + i), rstd2.ap, ALU.mult, ALU.mult, [hbuf, gc, rstd2], [ys])
                T.dma("act", yT[i * 128:(i + 1) * 128, q0:q0 + 512], ys.ap, reads=[ys], sem_buf=ys)
        wtask(None, t_g)


    KM_v = KM.ap.rearrange("p (j h t) -> p j h t", j=2, h=4)
    VM_v = VM.ap.rearrange("p (j t n) -> p j t n", j=2, t=2)
    for mj in range(2 if KNOB["mem"] else 0):
        build_xg(memT, mj * 256, 256, G_MEM)
        tok_rstd(2)
        for grp in range(4):
            ws = wring.next()
            wv = ws.ap[:, 0:16 * 256].rearrange("p (k n) -> p k n", n=256)
            T.dma("pool", wv, w_mem_kv[:, grp * 256:(grp + 1) * 256].rearrange("(k p) n -> p k n", p=128),
                  writes=[ws], sem_buf=ws)
            if grp < 2:
                for jl in range(2):
                    h = grp * 2 + jl
                    ps = ps_ring.next()
                    for kc in range(16):
                        MM(ps, ps.ap[:, 0:256], wv[:, kc, jl * 128:(jl + 1) * 128], xg_v[:, kc, 0:256], kc == 0, kc == 15,
                           [ws, xg], kc == 15)
                    TT(KM_v[:, mj, h, :], ps.ap[:, 0:256], rstd_x.ap[:, 0:256], ALU.mult, [ps, rstd_x], [KM])
            else:
                for tt in range(2):
                    ps = ps_ring.next()
                    for kc in range(16):
                        MM(ps, ps.ap[:, 0:256], xg_v[:, kc, tt * 128:(tt + 1) * 128], wv[:, kc, :], kc == 0, kc == 15,
                           [ws, xg], kc == 15)
                    TS(VM_v[:, mj, tt, (grp - 2) * 256:(grp - 1) * 256], ps.ap[:, 0:256], rstd_t.ap[:, tt:tt + 1], ALU.mult,
                       [ps, rstd_t], [VM])


    make_job(0)
    ltasks = [t for t in tasks if t[0] is not None]
    tasks.clear()
    Wc = nc.dram_tensor("Wc", [len(ltasks), 128, SLOTW], BF16).ap()
    Wc_bufs = [Buf("Wc%d" % i) for i in range(len(ltasks))]

    def wstore(i, ws):
        n = nused_tab[i]
        T.dma("pool", Wc[i, :, 0:n], ws.ap[:, 0:n], reads=[ws], writes=[Wc_bufs[i]], sem_buf=ws)

    apos[0] = max(apos[0], s1_end)
    cring = wring
    N1 = min(KNOB["n1"], len(ltasks))

    tail_slot = []

    def convert_range(i0, i1, ring):
        WMODE[0] = "convert"
        prev = None
        for i in range(i0, i1):
            if tail_slot and i >= i1 - KNOB["ntail"]:
                if prev is not None:
                    wstore(*prev)
                    prev = None
                ws = tail_slot[0]
                ltasks[i][0](ws)
                wstore(i, ws)
                continue
            ws = ring.next()
            ltasks[i][0](ws)
            if prev is not None:
                wstore(*prev)
            prev = (i, ws)
        if prev is not None:
            wstore(*prev)
        WMODE[0] = "cached"

    if KNOB["njob"] > 0:
        convert_range(0, N1, wring)
    assert xring.bufs[1].off == xring.bufs[0].off + 2048
    cvX = Buf("cvX", arena[:, xring.bufs[0].off:xring.bufs[0].off + SLOTW // 2].bitcast(BF16))
    cvX.aliases = list(xring.bufs)
    gl = gates + [rstd2] + yring.bufs
    assert all(gl[k + 1].off == gl[k].off + 512 for k in range(5))
    cvG = Buf("cvG", arena[:, gl[0].off:gl[0].off + SLOTW // 2].bitcast(BF16))
    cvG.aliases = gl
    cv_ring = Ring([cvX, cvG])
    conv2.append(lambda: (tail_slot.append(cvX), convert_range(N1, len(ltasks), cv_ring)))
    WMODE[0] = "cached"
    apos[0] = max(apos[0], s1_end)

    cur["xring"] = xring_s1
    for cb in range(KNOB["nctx"]):
        c0 = cb * 512
        cur.update(CUR[cb % 2])
        xgb, rx, rx2, rt = cur["xg"], cur["rstd_x"], cur["rstd_x2"], cur["rstd_t"]
        xgv = xgb.ap.rearrange("p (k n) -> p k n", n=512)
        build_xg(xc, c0, 512, G_MIX)
        T.dma("sp", tabA.ap.rearrange("p (a n) -> p a n", n=512), tabA_c[:, :, c0:c0 + 512], writes=[tabA], sem_buf=tabA)
        T.dma("sp", tabB.ap.rearrange("p (a n) -> p a n", n=512), tabB_c[:, :, c0:c0 + 512], writes=[tabB], sem_buf=tabB)
        qns = []
        for h in range(2):
            ps = ps_ring.next()
            for kc in range(16):
                MM(ps, ps.ap, wkv_v[:, kc, h * 128:(h + 1) * 128], xgv[:, kc, :], kc == 0, kc == 15, [wkv, xgb], kc == 15)
            qns.append(headnorm_part1(ps, G_KA))
        tok_rstd(4)
        va_v = va_st.ap.rearrange("p (h t d) -> p h t d", h=2, t=4)
        for tt in range(4):
            ps = ps_ring.next()
            for kc in range(16):
                MM(ps, ps.ap[:, 0:256], xgv[:, kc, tt * 128:(tt + 1) * 128], wkv_v[:, kc, 256:512], kc == 0, kc == 15,
                   [wkv, xgb], kc == 15)
            TS(va_v[:, :, tt, :], ps.ap[:, 0:256].rearrange("p (h d) -> p h d", h=2), rt.ap[:, tt:tt + 1], ALU.mult,
               [ps, rt], [va_st])
        T.dma("act", VA[:, :, 4 * cb:4 * cb + 4, :].rearrange("h p t d -> p h t d"), va_v,
              reads=[va_st], writes=[VA_b], sem_buf=va_st)
        sqs = []
        for i in range(2):
            ps = ps_ring.next()
            for kc in range(16):
                MM(ps, ps.ap, wkv_v[:, kc, 512 + i * 128:512 + (i + 1) * 128], xgv[:, kc, :], kc == 0, kc == 15,
                   [wkv, xgb], kc == 15)
            sq = sqring.next()
            ACT(sq.ap, ps.ap, AF.Square, [ps], [sq], scale=float(256 ** -0.5))
            ACT(craw.ap[:, i * 512:(i + 1) * 512], ps.ap, AF.Copy, [ps], [craw])
            sqs.append(sq)
        for h in range(2):
            rope(qns[h], qns[h].ap, tabA, tabA.ap[:, 0:512], tabA.ap[:, 512:1024], RA, ka_st.ap[:, h * 512:(h + 1) * 512], [ka_st])
        T.dma("act", KA[:, :, c0:c0 + 512].rearrange("h d t -> d h t"), ka_st.ap.rearrange("p (h t) -> p h t", h=2),
              reads=[ka_st], writes=[KA_b], sem_buf=ka_st)
        pss = ps_ring.next()
        for i in range(2):
            MM(pss, pss.ap, ones_b.ap, sqs[i].ap, i == 0, i == 1, [ones_b, sqs[i]], i == 1)
        u = tmpf.next()
        TT(u.ap, pss.ap, rx2.ap, ALU.mult, [pss, rx2], [u])
        rsq(u, u.ap, u.ap, [u])
        TT(u.ap, u.ap, rx.ap, ALU.mult, [u, rx], [u])
        for i in range(2):
            STT(ckvn.ap[:, i * 512:(i + 1) * 512], craw.ap[:, i * 512:(i + 1) * 512], gcol(G_CKV + i), u.ap, ALU.mult, ALU.mult,
                [craw, gc, u], [ckvn])
        ps = ps_ring.next()
        for kc in range(16):
            MM(ps, ps.ap, wkv_v[:, kc, 768:896], xgv[:, kc, :], kc == 0, kc == 15, [wkv, xgb], kc == 15)
        kr = tmpb.next()
        TT(kr.ap, ps.ap, rx.ap, ALU.mult, [ps, rx], [kr])
        ckvn_v = ckvn.ap.rearrange("p (k n) -> p k n", n=512)
        for h in range(8):
            ps = ps_ring.next()
            for kc in range(2):
                MM(ps, ps.ap, wkvb_h[:, kc, h, 0:128], ckvn_v[:, kc, :], kc == 0, kc == 1, [wkvb, ckvn], kc == 1)
            cp_evac(h, kb_st.ap[:, h * 512:(h + 1) * 512], ps.ap, [ps], [kb_st])
            if h == 1:
                rope(kr, kr.ap, tabB, tabB.ap[:, 0:512], tabB.ap[:, 512:1024], RB, kpe_st.ap, [kpe_st])
                TS(kpe2_st.ap[:, 0:512], kpe_st.ap, gcol(G_MLO), ALU.mult, [kpe_st, gc], [kpe2_st])
                TS(kpe2_st.ap[:, 512:1024], kpe_st.ap, gcol(G_MHI), ALU.mult, [kpe_st, gc], [kpe2_st])
                T.dma("act", KPE[:, :, c0:c0 + 512].rearrange("v p t -> p v t"), kpe2_st.ap.rearrange("p (v t) -> p v t", v=2),
                      reads=[kpe2_st], writes=[KPE_b], sem_buf=kpe2_st)
        T.dma("act", KB[:, :, c0:c0 + 512].rearrange("h d t -> d h t"), kb_st.ap.rearrange("p (h t) -> p h t", h=8),
              reads=[kb_st], writes=[KB_b], sem_buf=kb_st)
        vb_v = vb_st.ap.rearrange("p (h t d) -> p h t d", h=8, t=4)
        for tt in range(4):
            for hgi in range(2):
                ps = ps_ring.next()
                for kc in range(2):
                    MM(ps, ps.ap.rearrange("p (h d) -> p h d", h=4), ckvn_v[:, kc, tt * 128:(tt + 1) * 128],
                       wkvb_h[:, kc, 4 * hgi:4 * hgi + 4, 128:256], kc == 0, kc == 1, [wkvb, ckvn], kc == 1)
                cp_evac(tt * 2 + hgi + 1, vb_v[:, 4 * hgi:4 * hgi + 4, tt, :], ps.ap.rearrange("p (h d) -> p h d", h=4), [ps], [vb_st])
        T.dma("act", VB[:, :, 4 * cb:4 * cb + 4, :].rearrange("h p t d -> p h t d"), vb_v,
              reads=[vb_st], writes=[VB_b], sem_buf=vb_st)
    cur.update(CUR[0])
    cur["xring"] = xring

    fence()
    for jb in range(KNOB["njob"]):
        make_job(jb)
    run_tasks()
    T.final_wait("sp")
    T.emit()
    return nc


def _rope_tables(rows, cols, kind):
    d = np.arange(128)
    if kind == "A":
        part = d // 64
        i = d % 32
        f = (10000.0 ** (-(np.arange(0, 64, 2, dtype=np.float32)) / 64)).astype(np.float32)[i]
        first = (d % 64) < 32
    else:
        dd = d % 64
        part = dd // 32
        i = dd % 16
        f = (10000.0 ** (-(np.arange(0, 32, 2, dtype=np.float32)) / 32)).astype(np.float32)[i]
        first = (dd % 32) < 16
    pos = np.where(part[:, None] == 0, rows[None, :], cols[None, :]).astype(np.float32)
    ang = (pos * f[:, None]).astype(np.float32)
    c = np.cos(ang.astype(np.float64))
    s = np.sin(ang.astype(np.float64)) * np.where(first, -1.0, 1.0)[:, None]
    return np.ascontiguousarray(np.stack([c, s], axis=1).astype(np.float32))


def _rmats():
    d = np.arange(128)
    pa = np.where((d % 64) < 32, d + 32, d - 32)
    pb = np.where((d % 32) < 16, d + 16, d - 16)
    R = np.zeros((128, 256), np.float32)
    R[pa, d] = 1.0
    R[pb, 128 + d] = 1.0
    return R


_NC_CACHE = {}


def prep_inputs(x_prompt, x_sample, mem_prompt, mem_sample, g_mix, w_in, g_qa, g_ka, g_cq, w_q_b, g_ckv, w_kv_b,
                g_mem, w_mem_kv, w_br_a, w_br_b, w_br_m, w_out, g_mlp, w_up, w_down, g_final, cores=range(NCORE)):
    f = lambda a: np.ascontiguousarray(np.asarray(a, dtype=np.float32))
    x_prompt, x_sample, mem_prompt, mem_sample = f(x_prompt), f(x_sample), f(mem_prompt), f(mem_sample)
    gcols = np.zeros((128, NG), np.float32)
    gcols[:, G_MIX:G_MIX + 16] = f(g_mix).reshape(16, 128).T
    gcols[:, G_MLP:G_MLP + 16] = f(g_mlp).reshape(16, 128).T
    gcols[:, G_FIN:G_FIN + 16] = f(g_final).reshape(16, 128).T
    gcols[:, G_MEM:G_MEM + 16] = f(g_mem).reshape(16, 128).T
    gcols[:, G_QA] = f(g_qa).reshape(128)
    gcols[:, G_KA] = f(g_ka).reshape(128)
    gcols[:, G_CQ:G_CQ + 4] = f(g_cq).reshape(4, 128).T
    gcols[:, G_CKV:G_CKV + 2] = f(g_ckv).reshape(2, 128).T
    gcols[:, G_EPS] = EPS
    gcols[0:64, G_MLO] = 1.0
    gcols[64:128, G_MHI] = 1.0
    shared = {
        "rmat": _rmats(), "gcols": gcols,
        "w_in": f(w_in)[0], "w_q_b": f(w_q_b)[0], "w_kv_b": f(w_kv_b)[0], "w_mem_kv": f(w_mem_kv)[0],
        "w_br_a": f(w_br_a)[0], "w_br_b": f(w_br_b)[0], "w_br_m": f(w_br_m)[0], "w_out": f(w_out)[0],
        "w_up": f(w_up)[0], "w_down": f(w_down)[0],
    }
    tp = np.arange(LP)
    ts = np.arange(LS)
    in_maps = []
    xpT = np.ascontiguousarray(x_prompt[0].T)
    for c in cores:
        b, hf = c // 2, c % 2
        xsT = x_sample[b].T
        qp = np.arange(1024 * c, 1024 * (c + 1))
        qs = np.arange(1024 * hf, 1024 * (hf + 1))
        m = dict(shared)
        m["xq"] = np.ascontiguousarray(np.concatenate([xpT[:, qp], xsT[:, qs]], axis=1))
        m["xc"] = np.ascontiguousarray(np.concatenate([xpT, xsT], axis=1))
        m["memT"] = np.ascontiguousarray(np.concatenate([mem_prompt[0].T, mem_sample[b].T], axis=1))
        crow = np.concatenate([tp // 64, ts // 64])
        ccol = np.concatenate([tp % 64, ts % 64])
        qrow = np.concatenate([qp // 64, qs // 64])
        qcol = np.concatenate([qp % 64, qs % 64])
        m["tabA_c"] = _rope_tables(crow, ccol, "A")
        m["tabB_c"] = _rope_tables(crow, ccol, "B")
        m["tabA_q"] = _rope_tables(qrow, qcol, "A")
        m["tabB_q"] = _rope_tables(qrow, qcol, "B")
        in_maps.append(m)
    return in_maps


def kernel(**inputs):
    in_maps = prep_inputs(**inputs)
    if "nc" not in _NC_CACHE:
        _NC_CACHE["nc"] = build_program()
    nc = _NC_CACHE["nc"]
    res = run_bass_kernel_spmd(nc, in_maps, core_ids=list(range(NCORE)))
    y_prompt = np.empty((1, LP, D), np.float32)
    y_sample = np.empty((4, LS, D), np.float32)
    for c in range(NCORE):
        b, hf = c // 2, c % 2
        yt = res.results[c]["yT"]
        y_prompt[0, 1024 * c:1024 * (c + 1), :] = yt[:, 0:1024].T
        y_sample[b, 1024 * hf:1024 * (hf + 1), :] = yt[:, 1024:2048].T
    return (y_prompt, y_sample)
```

```python
import numpy as np
import concourse.bass as bass
import concourse.mybir as mybir
from concourse.bass_utils import run_bass_kernel_spmd

F32 = mybir.dt.float32
BF16 = mybir.dt.bfloat16
ALU = mybir.AluOpType
AF = mybir.ActivationFunctionType

ENGS = ("pe", "act", "dve", "pool", "sp")
D = 2048
NCORE = 8
LP, LS = 8192, 2048
LC = LP + LS
TQ = 512
NJOB = 4
EPS = 1e-6
O_QA, O_KA, O_VA, O_CQ, O_CKV, O_KR, O_QM, O_G = 0, 1024, 1280, 1536, 2048, 2304, 2368, 2880
G_MIX, G_MLP, G_FIN, G_MEM, G_QA, G_KA, G_CQ, G_CKV, G_EPS, G_MLO, G_MHI, NG = 0, 16, 32, 48, 64, 65, 66, 70, 72, 73, 74, 75
SLOTW = 6144
DEBUG = False
KNOB = {"ntail": 9, "n1": 41, "lag": 4, "pool_den": True, "s5": 9, "s1": 99, "nctx": LC // 512, "mem": True, "njob": NJOB, "phase": "g"}


class Buf:
    def __init__(self, name, ap=None, accum=False):
        self.name = name
        self.ap = ap
        self.w = {}
        self.r = {}
        self.accum = accum
        self.dsem = None
        self.psum = False
        self.aliases = []


class Tracker:
    def __init__(self, nc):
        self.nc = nc
        self.q = {e: [] for e in ENGS}
        self.cnt = {e: 0 for e in ENGS}
        self.waited = {e: {} for e in ENGS}
        self.dtotal = {}
        self.sems = {}
        self.ndsem = 0
        self.self_wait = True

    def sem(self, name):
        if name not in self.sems:
            self.sems[name] = self.nc.alloc_semaphore(name)
        return self.sems[name]

    def _dsem_for(self, buf, queue):
        kind = "sw" if queue == "pool" else "hw"
        if buf.dsem is None:
            buf.dsem = {}
        if kind not in buf.dsem:
            name = "d%d" % self.ndsem
            self.ndsem += 1
            self.dtotal[name] = 0
            self.sem(name)
            buf.dsem[kind] = name
        return buf.dsem[kind]

    def _wait(self, eng, toks):
        for s, v in toks:
            if s in self.dtotal:
                pass
            else:
                if s == eng and (eng == "pe" or not self.self_wait):
                    continue
                assert v <= self.cnt[s], (
                    "wait on unissued milestone %s %d > %d (eng %s)" % (s, v, self.cnt[s], eng))
            if v <= 0 or self.waited[eng].get(s, 0) >= v:
                continue
            self.waited[eng][s] = v
            self.q[eng].append(("wait", s, v))

    def _deps(self, reads, writes, eng=None):
        toks = []
        for b in reads:
            toks.extend(b.w.items())
            if b.psum:
                toks.extend(b.r.items())
        for b in writes:
            toks.extend(b.w.items())
            toks.extend(b.r.items())
        return toks

    @staticmethod
    def _expand(bufs):
        return [x for b in bufs for x in [b] + b.aliases]

    def op(self, eng, fn, reads=(), writes=(), inc=True):
        reads, writes = self._expand(reads), self._expand(writes)
        self._wait(eng, self._deps(reads, writes, eng))
        if inc:
            self.cnt[eng] += 1
            v = self.cnt[eng]
        else:
            v = self.cnt[eng] + 1
        self.q[eng].append(("op", fn, eng if inc else None, 1))
        for b in reads:
            b.r[eng] = max(b.r.get(eng, 0), v)
        for b in writes:
            if b.accum:
                b.w[eng] = max(b.w.get(eng, 0), v)
            else:
                b.w = {eng: v}
                b.r = {}

    def dma(self, queue, out_ap, in_ap, reads=(), writes=(), sem_buf=None):
        reads, writes = self._expand(reads), self._expand(writes)
        self._wait(queue, self._deps(reads, writes, queue))
        s = self._dsem_for(sem_buf, queue)
        self.dtotal[s] += 16
        v = self.dtotal[s]
        self.q[queue].append(("op", (lambda e, o=out_ap, i=in_ap: e.dma_start(out=o, in_=i)), s, 16))
        for b in reads:
            b.r[s] = v
        for b in writes:
            if b.accum:
                b.w[s] = v
            else:
                b.w = {s: v}
                b.r = {}

    def final_wait(self, eng="sp"):
        toks = [(e, self.cnt[e]) for e in ("pe", "act", "dve", "pool")]
        toks += [(s, v) for s, v in self.dtotal.items()]
        self._wait(eng, [t for t in toks if t[0] != eng])

    def emit(self):
        nc = self.nc
        for e in ("pe", "act", "dve", "pool"):
            self.sem(e)
        engmap = {"pe": "tensor", "act": "scalar", "dve": "vector", "pool": "gpsimd", "sp": "sync"}
        with nc.Block() as block:
            for e in ENGS:
                ops = self.q[e]

                def body(h, ops=ops):
                    for o in ops:
                        if o[0] == "wait":
                            h.wait_ge(self.sems[o[1]], o[2])
                        else:
                            ins = o[1](h)
                            if o[2] is not None:
                                ins.then_inc(self.sems[o[2]], o[3])

                getattr(block, engmap[e])(body)


class Ring:
    def __init__(self, bufs):
        self.bufs = bufs
        self.i = 0

    def next(self):
        b = self.bufs[self.i % len(self.bufs)]
        self.i += 1
        return b


def build_program():
    nc = bass.Bass("TRN2", target_bir_lowering=False)
    T = Tracker(nc)

    def din(name, shape):
        return nc.dram_tensor(name, list(shape), F32, kind="ExternalInput").ap()

    xq = din("xq", [D, NJOB * TQ])
    xc = din("xc", [D, LC])
    memT = din("memT", [D, 512])
    tabA_c = din("tabA_c", [128, 2, LC])
    tabB_c = din("tabB_c", [128, 2, LC])
    tabA_q = din("tabA_q", [128, 2, NJOB * TQ])
    tabB_q = din("tabB_q", [128, 2, NJOB * TQ])
    rmat = din("rmat", [128, 256])
    gcols_d = din("gcols", [128, NG])
    w_in = din("w_in", [D, 9024])
    w_q_b = din("w_q_b", [512, 1536])
    w_kv_b = din("w_kv_b", [256, 2048])
    w_mem_kv = din("w_mem_kv", [D, 1024])
    w_br_a = din("w_br_a", [1024, D])
    w_br_b = din("w_br_b", [1024, D])
    w_br_m = din("w_br_m", [512, D])
    w_out = din("w_out", [D, D])
    w_up = din("w_up", [D, 4 * D])
    w_down = din("w_down", [4 * D, D])
    yT = nc.dram_tensor("yT", [D, NJOB * TQ], F32, kind="ExternalOutput").ap()

    okind = "ExternalOutput" if DEBUG else "Internal"
    KA = nc.dram_tensor("KA", [2, 128, LC], BF16, kind=okind).ap()
    VA = nc.dram_tensor("VA", [2, 128, LC // 128, 128], BF16, kind=okind).ap()
    KB = nc.dram_tensor("KB", [8, 128, LC], BF16, kind=okind).ap()
    KPE = nc.dram_tensor("KPE", [2, 128, LC], BF16, kind=okind).ap()
    VB = nc.dram_tensor("VB", [8, 128, LC // 128, 128], BF16, kind=okind).ap()
    KA_b, VA_b, KB_b, KPE_b, VB_b = (Buf(n, accum=True) for n in ("KA", "VA", "KB", "KPE", "VB"))

    NWORDS = 53000
    arena = nc.alloc_sbuf_tensor("arena", [128, NWORDS], F32)
    apos = [0]

    def alloc_f(name, n):
        o = apos[0]
        apos[0] += n
        assert apos[0] <= NWORDS, "SBUF arena overflow at %s: %d" % (name, apos[0])
        b = Buf(name, arena[:, o:o + n])
        b.off = o
        return b

    def alloc_b(name, n):
        assert n % 2 == 0
        o = apos[0]
        apos[0] += n // 2
        assert apos[0] <= NWORDS, "SBUF arena overflow at %s: %d" % (name, apos[0])
        b = Buf(name, arena[:, o:o + n // 2].bitcast(BF16))
        b.off = o
        return b

    ps_t = [nc.alloc_psum_tensor("psb%d" % i, [128, 512], F32) for i in range(8)]
    PSB = [Buf("ps%d" % i, ps_t[i][:, :]) for i in range(8)]
    for b_ in PSB:
        b_.psum = True
    ps_acc = PSB[0:2]
    ps_ring = Ring(PSB[2:8])

    ones_b = alloc_b("ones_b", 128)
    ones_f = alloc_f("ones_f", 128)
    rm_b = alloc_b("rm_b", 256)
    gc = alloc_f("gcols", NG + 3)
    KM = alloc_b("KM", 2 * 4 * 256)
    VM = alloc_b("VM", 2 * 2 * 512)
    wring = Ring([alloc_b("wslot%d" % i, SLOTW) for i in range(3)])
    xring = Ring([alloc_f("xslot%d" % i, 4 * 512) for i in range(2)])
    sqring = Ring([alloc_b("sq%d" % i, 512) for i in range(4)])
    tmpf = Ring([alloc_f("tmpf%d" % i, 512) for i in range(5)])
    tmpb = Ring([alloc_b("tmpb%d" % i, 512) for i in range(3)])
    rstd_x = alloc_f("rstd_x", 512)
    rstd_x2 = alloc_f("rstd_x2", 512)
    rstd_t = alloc_f("rstd_t", 8)
    tabA = alloc_f("tabA", 1024)
    tabB = alloc_f("tabB", 1024)
    xg = alloc_b("xg", 16 * 512)
    base_pos = apos[0]

    def gcol(i):
        return gc.ap[:, i:i + 1]

    epsc = gc.ap[:, G_EPS:G_EPS + 1]

    def MM(psb, out_ap, lhsT, rhs, start, stop, reads, last):
        T.op("pe", lambda e: e.matmul(out_ap, lhsT, rhs, start=start, stop=stop),
             reads=reads, writes=[psb], inc=last)

    def ACT(out_ap, in_ap, func, reads, writes, **kw):
        T.op("act", lambda e: e.activation(out=out_ap, in_=in_ap, func=func, **kw), reads=reads, writes=writes)

    def TT(out_ap, in0, in1, op, reads, writes, eng="dve"):
        T.op(eng, lambda e: e.tensor_tensor(out=out_ap, in0=in0, in1=in1, op=op), reads=reads, writes=writes)

    def STT(out_ap, in0, scalar, in1, op0, op1, reads, writes):
        T.op("dve", lambda e: e.scalar_tensor_tensor(out=out_ap, in0=in0, scalar=scalar, in1=in1, op0=op0, op1=op1),
             reads=reads, writes=writes)

    def TS(out_ap, in0, s1, op0, reads, writes, s2=None, op1=None):
        if op1 is None:
            T.op("dve", lambda e: e.tensor_scalar(out=out_ap, in0=in0, scalar1=s1, scalar2=None, op0=op0),
                 reads=reads, writes=writes)
        else:
            T.op("dve", lambda e: e.tensor_scalar(out=out_ap, in0=in0, scalar1=s1, scalar2=s2, op0=op0, op1=op1),
                 reads=reads, writes=writes)

    def rsq(outb, out_ap, in_ap, in_bufs):
        ACT(out_ap, in_ap, AF.Ln, reads=in_bufs + [gc], writes=[outb], bias=epsc, scale=1.0)
        ACT(out_ap, out_ap, AF.Exp, reads=[outb], writes=[outb], scale=-0.5)

    def cp_evac(k, out_ap, in_ap, reads, writes):
        if k % 2 == 0:
            ACT(out_ap, in_ap, AF.Copy, reads=reads, writes=writes)
        else:
            T.op("dve", lambda e: e.tensor_copy(out=out_ap, in_=in_ap), reads=reads, writes=writes)

    T.op("dve", lambda e: e.memset(ones_b.ap, 1.0), writes=[ones_b])
    T.op("dve", lambda e: e.memset(ones_f.ap, 1.0), writes=[ones_f])
    T.dma("pool", rm_b.ap, rmat, writes=[rm_b], sem_buf=rm_b)
    T.dma("sp", gc.ap[:, 0:NG], gcols_d, writes=[gc], sem_buf=gc)
    RA = rm_b.ap[:, 0:128]
    RB = rm_b.ap[:, 128:256]

    def rope(xb_buf, xb_ap, tab_buf, c_ap, s_ap, R, out_ap, out_bufs, n=512):
        pr = ps_ring.next()
        MM(pr, pr.ap[:, 0:n], R, xb_ap, True, True, [rm_b, xb_buf], True)
        t1 = tmpf.next()
        TT(t1.ap[:, 0:n], xb_ap, c_ap, ALU.mult, [xb_buf, tab_buf], [t1])
        t2 = tmpf.next()
        TT(t2.ap[:, 0:n], pr.ap[:, 0:n], s_ap, ALU.mult, [pr, tab_buf], [t2])
        TT(out_ap, t1.ap[:, 0:n], t2.ap[:, 0:n], ALU.add, [t1, t2], out_bufs)

    cur = {}

    def headnorm_part1(ps, gidx):
        rx, rx2 = cur["rstd_x"], cur["rstd_x2"]
        sq = sqring.next()
        ACT(sq.ap, ps.ap, AF.Square, [ps], [sq], scale=float(128 ** -0.5))
        pss = ps_ring.next()
        MM(pss, pss.ap, ones_b.ap, sq.ap, True, True, [ones_b, sq], True)
        u = tmpf.next()
        TT(u.ap, pss.ap, rx2.ap, ALU.mult, [pss, rx2], [u])
        rsq(u, u.ap, u.ap, [u])
        TT(u.ap, u.ap, rx.ap, ALU.mult, [u, rx], [u])
        qn = tmpb.next()
        STT(qn.ap, ps.ap, gcol(gidx), u.ap, ALU.mult, ALU.mult, [ps, gc, u], [qn])
        return qn

    def load_x_group(src, col0, ncol, g):
        xs = cur.get("xring", xring).next()
        v = xs.ap[:, 0:4 * ncol].rearrange("p (a n) -> p a n", n=ncol)
        T.dma("sp", v, src[g * 512:(g + 1) * 512, col0:col0 + ncol].rearrange("(a p) n -> p a n", p=128),
              writes=[xs], sem_buf=xs)
        return xs, v

    def build_xg(src, col0, ncol, gbase):
        xgb, rx, rx2 = cur["xg"], cur["rstd_x"], cur["rstd_x2"]
        pss = ps_ring.next()
        for g in range(4):
            xs, v = load_x_group(src, col0, ncol, g)
            for j in range(4):
                kc = 4 * g + j
                sq = sqring.next()
                ACT(sq.ap[:, 0:ncol], v[:, j, :], AF.Square, [xs], [sq], scale=float(D ** -0.5))
                MM(pss, pss.ap[:, 0:ncol], ones_b.ap, sq.ap[:, 0:ncol], kc == 0, kc == 15, [ones_b, sq], True)
                TS(xgb.ap[:, kc * 512:kc * 512 + ncol], v[:, j, :], gcol(gbase + kc), ALU.mult, [xs, gc], [xgb])
        rsq(rx, rx.ap[:, 0:ncol], pss.ap[:, 0:ncol], [pss])
        TT(rx2.ap[:, 0:ncol], rx.ap[:, 0:ncol], rx.ap[:, 0:ncol], ALU.mult, [rx], [rx2])

    def tok_rstd(ntt):
        rx, rt = cur["rstd_x"], cur["rstd_t"]
        pt = ps_ring.next()
        for tt in range(ntt):
            MM(pt, pt.ap[:, tt:tt + 1], rx.ap[0:1, tt * 128:(tt + 1) * 128], ones_f.ap[0:1, 0:1],
               True, True, [rx, ones_f], tt == ntt - 1)
        T.op("dve", lambda e: e.tensor_copy(out=rt.ap[:, 0:ntt], in_=pt.ap[:, 0:ntt]), reads=[pt], writes=[rt])

    m1 = apos[0]
    wkv = alloc_b("wkv", 16 * 896)
    wkvb = alloc_b("wkvb", 2 * 2048)
    craw = alloc_f("craw", 2 * 512)
    ckvn = alloc_b("ckvn", 2 * 512)
    ka_st = alloc_b("ka_st", 2 * 512)
    va_st = alloc_b("va_st", 2 * 4 * 128)
    kb_st = alloc_b("kb_st", 8 * 512)
    vb_st = alloc_b("vb_st", 8 * 4 * 128)
    kpe_st = alloc_b("kpe_st", 512)
    kpe2_st = alloc_b("kpe2_st", 2 * 512)
    xg2 = alloc_b("xg2", 16 * 512)
    rstd_xb = alloc_f("rstd_xb", 512)
    rstd_x2b = alloc_f("rstd_x2b", 512)
    rstd_tb = alloc_f("rstd_tb", 8)
    nxs = min(4, (NWORDS - apos[0]) // 2048)
    xring_s1 = Ring(xring.bufs + [alloc_f("xslot_s1_%d" % i, 4 * 512) for i in range(nxs)])
    CUR = [dict(xg=xg, rstd_x=rstd_x, rstd_x2=rstd_x2, rstd_t=rstd_t),
           dict(xg=xg2, rstd_x=rstd_xb, rstd_x2=rstd_x2b, rstd_t=rstd_tb)]
    cur.update(CUR[0])

    wkv_v = wkv.ap.rearrange("p (k n) -> p k n", n=896)
    for (dst0, src0, n) in ((0, O_KA, 256), (256, O_VA, 256), (512, O_CKV, 256), (768, O_KR, 64), (832, O_KR, 64)):
        T.dma("pool", wkv_v[:, :, dst0:dst0 + n], w_in[:, src0:src0 + n].rearrange("(k p) n -> p k n", p=128),
              writes=[wkv], sem_buf=wkv)
    wkvb_v = wkvb.ap.rearrange("p (k n) -> p k n", n=2048)
    T.dma("pool", wkvb_v, w_kv_b.rearrange("(k p) n -> p k n", p=128), writes=[wkvb], sem_buf=wkvb)
    wkvb_h = wkvb.ap.rearrange("p (k h n) -> p k h n", k=2, h=8)
    xg_v = xg.ap.rearrange("p (k n) -> p k n", n=512)

    def fence():
        wsems = set(n for b in list(wring.bufs) + [cvX, cvG] for n in (b.dsem or {}).values())
        toks = [(e, T.cnt[e]) for e in ("pe", "act", "dve", "pool")] + [(k, v) for k, v in T.dtotal.items() if k not in wsems]
        for e in ENGS:
            T._wait(e, [t for t in toks if t[0] != e])

    s1_end = apos[0]
    apos[0] = m1
    oT = alloc_b("oT", 20 * 512)
    gates = [alloc_f("gate%d" % i, 512) for i in range(3)]
    rstd2 = alloc_f("rstd2", 512)
    yring = Ring([alloc_f("yst%d" % i, 512) for i in range(2)])
    z0 = apos[0]
    hbuf = alloc_f("hbuf", 16 * 512)
    hg = alloc_b("hg", 16 * 512)
    merged = alloc_b("merged", 16 * 512)
    z1 = apos[0]
    apos[0] = z1 - 4096
    aq = alloc_b("aq", 16 * 512)
    apos[0] = z0
    qa = alloc_b("qa", 8 * 512)
    qpe = alloc_b("qpe", 4 * 512)
    cqn = alloc_b("cqn", 4 * 512)
    cqraw = alloc_f("cqraw", 4 * 512)
    NACC = 4
    daccs = [alloc_f("dacc%d" % i, 512) for i in range(NACC)]
    rden = alloc_f("rden", 512)
    kvring = Ring([(alloc_b("kc%d" % i, 1024), alloc_b("pc%d" % i, 1024), alloc_b("vc%d" % i, 1024)) for i in range(3)])
    pring = Ring([alloc_b("pT%d" % i, 512) for i in range(6)])
    assert apos[0] <= z1, (apos[0], z1)
    apos[0] = z1
    qa_v = qa.ap.rearrange("p (h n) -> p h n", n=512)
    qpe_v = qpe.ap.rearrange("p (h n) -> p h n", n=512)
    cqn_v = cqn.ap.rearrange("p (h n) -> p h n", n=512)
    oT_v = oT.ap.rearrange("p (h n) -> p h n", n=512)
    mg_v = merged.ap.rearrange("p (h n) -> p h n", n=512)
    hb_v = hbuf.ap.rearrange("p (h n) -> p h n", n=512)
    hg_v = hg.ap.rearrange("p (h n) -> p h n", n=512)
    aq_v = aq.ap.rearrange("p (h n) -> p h n", n=512)

    tasks = []

    jl_cnt = [0]
    cur_idx = [0]
    WMODE = ["convert"]
    nused_tab = {}

    def wtask(load, compute):
        if load is not None:
            idx = jl_cnt[0]
            jl_cnt[0] += 1

            def load2(ws, idx=idx, load=load):
                cur_idx[0] = idx
                load(ws)
            tasks.append((load2, compute))
        else:
            tasks.append((None, compute))

    def run_tasks(depth=2):
        n = len(tasks)
        slots = [None] * n
        lidx = [i for i in range(n) if tasks[i][0] is not None]
        nxt = 0
        for i in range(n):
            while nxt < len(lidx) and (lidx[nxt] <= i or sum(1 for j in lidx[:nxt] if j > i) < depth):
                k = lidx[nxt]
                slots[k] = wring.next()
                tasks[k][0](slots[k])
                nxt += 1
            tasks[i][1](slots[i])
        tasks.clear()

    def wload(ws, views, nused):
        if WMODE[0] == "convert":
            for d, s in views:
                T.dma("pool", d, s, writes=[ws], sem_buf=ws)
            nused_tab[cur_idx[0]] = nused
        else:
            i = cur_idx[0]
            T.dma("sp", ws.ap[:, 0:nused], Wc[i, :, 0:nused], reads=[Wc_bufs[i]], writes=[ws], sem_buf=ws)

    def proj16(wsrc, col0, ntile, consume, rhs_v, rhs_buf):
        i = 0
        pend_def = []
        while i < ntile:
            n = min(3, ntile - i)

            def load(ws, i=i, n=n):
                wv = ws.ap[:, 0:16 * n * 128].rearrange("p (k n) -> p k n", n=n * 128)
                wload(ws, [(wv, wsrc[:, col0 + i * 128:col0 + (i + n) * 128].rearrange("(k p) n -> p k n", p=128))], 16 * n * 128)

            def comp(ws, i=i, n=n):
                wv = ws.ap[:, 0:16 * n * 128].rearrange("p (k n) -> p k n", n=n * 128)
                for jl in range(n):
                    ps = ps_ring.next()
                    for kc in range(16):
                        MM(ps, ps.ap, wv[:, kc, jl * 128:(jl + 1) * 128], rhs_v[:, kc, :], kc == 0, kc == 15,
                           [ws, rhs_buf], kc == 15)
                    t = i + jl
                    due = sorted([e for e in pend_def if e[0] <= t], key=lambda e: e[0])
                    for e in due:
                        pend_def.remove(e)
                        e[1]()
                    d = consume(t, ps)
                    if d is not None:
                        if callable(d):
                            d = [(1, d)]
                        for delay, fn in d:
                            pend_def.append((t + delay, fn))
                if i + n >= ntile:
                    for e in sorted(pend_def, key=lambda e: e[0]):
                        e[1]()
                    del pend_def[:]

            wtask(load, comp)
            i += n

    def attention(kind, jb, c0, L, mj):
        nh = {"A": 8, "B": 8, "M": 4}[kind]
        obase = {"A": 0, "B": 8, "M": 16}[kind]
        scale = {"A": 128 ** -0.5, "B": 192 ** -0.5, "M": 128 ** -0.5}[kind]
        nch = 1 if kind == "M" else L // 1024
        steps = [(h, ch) for h in range(nh) for ch in range(nch)]
        loaded = {}

        def issue(si):
            h, ch = steps[si]
            kc_, pc_, vc_ = kvring.next()
            t0 = c0 + ch * 1024
            if kind == "A":
                T.dma("sp", kc_.ap, KA[h // 4, :, t0:t0 + 1024], reads=[KA_b], writes=[kc_], sem_buf=kc_)
                T.dma("sp", vc_.ap.rearrange("p (t d) -> p t d", d=128), VA[h // 4, :, t0 // 128:t0 // 128 + 8, :],
                      reads=[VA_b], writes=[vc_], sem_buf=vc_)
            else:
                T.dma("sp", kc_.ap, KB[h, :, t0:t0 + 1024], reads=[KB_b], writes=[kc_], sem_buf=kc_)
                T.dma("sp", pc_.ap, KPE[h % 2, :, t0:t0 + 1024], reads=[KPE_b], writes=[pc_], sem_buf=pc_)
                T.dma("sp", vc_.ap.rearrange("p (t d) -> p t d", d=128), VB[h, :, t0 // 128:t0 // 128 + 8, :],
                      reads=[VB_b], writes=[vc_], sem_buf=vc_)
            loaded[si] = (kc_, pc_, vc_)

        nissued = 0
        pend = []
        nd = [0]
        for si, (h, ch) in enumerate(steps):
            if kind != "M":
                while nissued < min(len(steps), si + 2):
                    issue(nissued)
                    nissued += 1
                kc_, pc_, vc_ = loaded.pop(si)
            nkt = 2 if kind == "M" else 8
            pso = ps_acc[0]
            for kt in range(nkt):
                first = (ch == 0 and kt == 0)
                last = (ch == nch - 1 and kt == nkt - 1)
                pss = ps_ring.next()
                if kind == "A":
                    MM(pss, pss.ap, kc_.ap[:, kt * 128:(kt + 1) * 128], qa_v[:, h, :], True, True, [kc_, qa], True)
                elif kind == "B":
                    MM(pss, pss.ap, kc_.ap[:, kt * 128:(kt + 1) * 128], qa_v[:, h, :], True, False, [kc_, qa], False)
                    MM(pss, pss.ap, pc_.ap[:, kt * 128:(kt + 1) * 128], qpe_v[:, h // 2, :], False, True,
                       [pc_, qpe], True)
                else:
                    MM(pss, pss.ap, KM_v[:, mj, h, kt * 128:(kt + 1) * 128], qa_v[:, h, :], True, True, [KM, qa], True)
                pT = pring.next()
                ACT(pT.ap, pss.ap, AF.Exp, [pss], [pT], scale=float(scale))
                ti = ch * nkt + kt
                pe_every = {"A": 3, "B": 4, "M": 0}[kind]
                if pe_every and ti % pe_every == 0:
                    dmode = "pe"
                else:
                    dmode = "dve"
                    a = daccs[nd[0] % NACC]
                    if nd[0] < NACC:
                        T.op("dve", lambda e, o=a.ap, i_=pT.ap: e.tensor_copy(out=o, in_=i_), reads=[pT], writes=[a])
                    else:
                        TT(a.ap, a.ap, pT.ap, ALU.add, [a, pT], [a])
                    nd[0] += 1
                if kind == "M":
                    vl, vb_ = VM_v[:, mj, kt, h * 128:(h + 1) * 128], VM
                else:
                    vl, vb_ = vc_.ap[:, kt * 128:(kt + 1) * 128], vc_
                pend.append((vl, vb_, pT, first, last, dmode == "pe"))

                def do_pv(ent):
                    vl2, vb2, pT2, f2, l2, pe_den = ent
                    MM(pso, pso.ap, vl2, pT2.ap, f2, l2, [vb2, pT2], l2)
                    if pe_den:
                        MM(ps_acc[1], ps_acc[1].ap, ones_b.ap, pT2.ap, f2, False, [ones_b, pT2], False)
                if len(pend) > KNOB["lag"]:
                    do_pv(pend.pop(0))
                if last:
                    while pend:
                        do_pv(pend.pop(0))
                    used = min(nd[0], NACC)
                    if pe_every:
                        psd = ps_acc[1]
                        for k in range(used):
                            MM(psd, psd.ap, ones_f.ap, daccs[k].ap, False, k == used - 1, [ones_f, daccs[k]], k == used - 1)
                    else:
                        psd = ps_ring.next()
                        for k in range(used):
                            MM(psd, psd.ap, ones_f.ap, daccs[k].ap, k == 0, k == used - 1, [ones_f, daccs[k]], k == used - 1)
                    nd[0] = 0
                    T.op("dve", lambda e, o=rden.ap, i_=psd.ap: e.reciprocal(out=o, in_=i_), reads=[psd], writes=[rden])
                    TT(oT_v[:, obase + h, :], pso.ap, rden.ap, ALU.mult, [pso, rden], [oT])

    conv2 = []

    def make_job(jb):
        jl_cnt[0] = 0
        q0 = jb * TQ
        PH = KNOB["phase"]
        is_p = jb < 2
        c0, L, mj = (0, LP, 0) if is_p else (LP, LS, 1)

        def t_a(ws, q0=q0):
            fence()
            build_xg(xq, q0, 512, G_MIX)
            T.dma("sp", tabA.ap.rearrange("p (a n) -> p a n", n=512), tabA_q[:, :, q0:q0 + 512], writes=[tabA], sem_buf=tabA)
            T.dma("sp", tabB.ap.rearrange("p (a n) -> p a n", n=512), tabB_q[:, :, q0:q0 + 512], writes=[tabB], sem_buf=tabB)
        wtask(None, t_a)
        if jb == 0 and conv2:
            wtask(None, lambda ws: conv2.pop()())

        def c_qa(i, ps):
            rx, rx2 = cur["rstd_x"], cur["rstd_x2"]
            sq = sqring.next()
            ACT(sq.ap, ps.ap, AF.Square, [ps], [sq], scale=float(128 ** -0.5))
            raw = cqraw.ap[:, (i % 4) * 512:(i % 4 + 1) * 512]
            ACT(raw, ps.ap, AF.Copy, [ps], [cqraw])
            qn_ap = cqn_v[:, i % 4, :]

            def stage1():
                pss = ps_ring.next()
                MM(pss, pss.ap, ones_b.ap, sq.ap, True, True, [ones_b, sq], True)
                u = tmpf.next()
                TT(u.ap, pss.ap, rx2.ap, ALU.mult, [pss, rx2], [u])
                rsq(u, u.ap, u.ap, [u])
                TT(u.ap, u.ap, rx.ap, ALU.mult, [u, rx], [u])
                STT(qn_ap, raw, gcol(G_QA), u.ap, ALU.mult, ALU.mult, [cqraw, gc, u], [cqn])

            def stage2():
                rope(cqn, qn_ap, tabA, tabA.ap[:, 0:512], tabA.ap[:, 512:1024], RA, qa_v[:, i, :], [qa])
            return [(1, stage1), (3, stage2)]
        proj16(w_in, O_QA, 8, c_qa, xg_v, xg)
        wtask(None, lambda ws, jb=jb, c0=c0, L=L, mj=mj: attention("A", jb, c0, L, mj))

        cq_sq = []

        def c_cq(i, ps, cq_sq=cq_sq):
            sq = alloc_sq[i]
            ACT(sq.ap, ps.ap, AF.Square, [ps], [sq], scale=float(512 ** -0.5))
            ACT(cqraw.ap[:, i * 512:(i + 1) * 512], ps.ap, AF.Copy, [ps], [cqraw])
            if i == 3:
                pss = ps_ring.next()
                for k in range(4):
                    MM(pss, pss.ap, ones_b.ap, alloc_sq[k].ap, k == 0, k == 3, [ones_b, alloc_sq[k]], k == 3)
                u = tmpf.next()
                TT(u.ap, pss.ap, rstd_x2.ap, ALU.mult, [pss, rstd_x2], [u])
                rsq(u, u.ap, u.ap, [u])
                TT(u.ap, u.ap, rstd_x.ap, ALU.mult, [u, rstd_x], [u])
                for k in range(4):
                    STT(cqn_v[:, k, :], cqraw.ap[:, k * 512:(k + 1) * 512], gcol(G_CQ + k), u.ap, ALU.mult, ALU.mult,
                        [cqraw, gc, u], [cqn])
        alloc_sq = sqring.bufs
        proj16(w_in, O_CQ, 4, c_cq, xg_v, xg)

        for half in range(2):
            def load(ws, half=half):
                wv = ws.ap[:, 0:4 * 768].rearrange("p (k n) -> p k n", n=768)
                wpe = ws.ap[:, 3072:4096].rearrange("p (k j n) -> p k j n", k=4, j=2)
                views = [(wv, w_q_b[:, half * 768:(half + 1) * 768].rearrange("(k p) n -> p k n", p=128))]
                for hl in range(4):
                    h = half * 4 + hl
                    views.append((wpe[:, :, hl // 2, (hl % 2) * 64:(hl % 2) * 64 + 64],
                                  w_q_b[:, h * 192 + 128:h * 192 + 192].rearrange("(k p) n -> p k n", p=128)))
                wload(ws, views, 4096)

            def comp(ws, half=half):
                wv = ws.ap[:, 0:4 * 768].rearrange("p (k h n) -> p k h n", k=4, h=4)
                wpe = ws.ap[:, 3072:4096].rearrange("p (k j n) -> p k j n", k=4, j=2)
                for hl in range(4):
                    h = half * 4 + hl
                    ps = ps_ring.next()
                    for kc in range(4):
                        MM(ps, ps.ap, wv[:, kc, hl, 0:128], cqn_v[:, kc, :], kc == 0, kc == 3, [ws, cqn], kc == 3)
                    cp_evac(hl, qa_v[:, h, :], ps.ap, [ps], [qa])
                for jl in range(2):
                    ps = ps_ring.next()
                    for kc in range(4):
                        MM(ps, ps.ap, wpe[:, kc, jl, :], cqn_v[:, kc, :], kc == 0, kc == 3, [ws, cqn], kc == 3)
                    qr = tmpb.next()
                    cp_evac(jl, qr.ap, ps.ap, [ps], [qr])
                    rope(qr, qr.ap, tabB, tabB.ap[:, 0:512], tabB.ap[:, 512:1024], RB, qpe_v[:, half * 2 + jl, :], [qpe])
            wtask(load, comp)
        wtask(None, lambda ws, jb=jb, c0=c0, L=L, mj=mj: attention("B", jb, c0, L, mj))

        def c_qm(i, ps):
            TT(qa_v[:, i, :], ps.ap, rstd_x.ap, ALU.mult, [ps, rstd_x], [qa])
        proj16(w_in, O_QM, 4, c_qm, xg_v, xg)
        wtask(None, lambda ws, jb=jb, c0=c0, L=L, mj=mj: attention("M", jb, c0, L, mj))

        wtask(None, lambda ws: fence())
        for j in range(16):
            def loadg(ws, j=j):
                wv = ws.ap[:, 0:16 * 384].rearrange("p (k b n) -> p k b n", k=16, b=3)
                wload(ws, [(wv[:, :, b, :],
                            w_in[:, O_G + b * D + j * 128:O_G + b * D + (j + 1) * 128].rearrange("(k p) n -> p k n", p=128))
                           for b in range(3)], 16 * 384)

            def compg(ws, j=j):
                wv = ws.ap[:, 0:16 * 384].rearrange("p (k b n) -> p k b n", k=16, b=3)
                for b in range(3):
                    ps = ps_ring.next()
                    for kc in range(16):
                        MM(ps, ps.ap, wv[:, kc, b, :], xg_v[:, kc, :], kc == 0, kc == 15, [ws, xg], kc == 15)
                    t = tmpf.next()
                    TT(t.ap, ps.ap, rstd_x.ap, ALU.mult, [ps, rstd_x], [t])
                    ACT(gates[b].ap, t.ap, AF.Sigmoid, [t], [gates[b]])

            def loadb(ws, j=j):
                wv = ws.ap[:, 0:20 * 128].rearrange("p (k n) -> p k n", n=128)
                wload(ws, [(wv[:, 0:8, :], w_br_a[:, j * 128:(j + 1) * 128].rearrange("(k p) n -> p k n", p=128)),
                           (wv[:, 8:16, :], w_br_b[:, j * 128:(j + 1) * 128].rearrange("(k p) n -> p k n", p=128)),
                           (wv[:, 16:20, :], w_br_m[:, j * 128:(j + 1) * 128].rearrange("(k p) n -> p k n", p=128))], 20 * 128)

            def compb(ws, j=j):
                wv = ws.ap[:, 0:20 * 128].rearrange("p (k n) -> p k n", n=128)
                ms = []
                for b, (k0, nk) in enumerate(((0, 8), (8, 8), (16, 4))):
                    ps = ps_ring.next()
                    for kc in range(nk):
                        MM(ps, ps.ap, wv[:, k0 + kc, :], oT_v[:, k0 + kc, :], kc == 0, kc == nk - 1, [ws, oT], kc == nk - 1)
                    m = tmpf.next()
                    TT(m.ap, ps.ap, gates[b].ap, ALU.mult, [ps, gates[b]], [m])
                    ms.append(m)
                TT(ms[0].ap, ms[0].ap, ms[1].ap, ALU.add, [ms[0], ms[1]], [ms[0]])
                TT(mg_v[:, j, :], ms[0].ap, ms[2].ap, ALU.add, [ms[0], ms[2]], [merged])
            wtask(loadg, compg)
            wtask(loadb, compb)

        def c_wo(i, ps, q0=q0):
            xs = xring.next()
            T.dma("sp", xs.ap[:, 0:512], xq[i * 128:(i + 1) * 128, q0:q0 + 512], writes=[xs], sem_buf=xs)
            TT(hb_v[:, i, :], ps.ap, xs.ap[:, 0:512], ALU.add, [ps, xs], [hbuf])
            sq = sqring.next()
            ACT(sq.ap, hb_v[:, i, :], AF.Square, [hbuf], [sq], scale=float(D ** -0.5))
            MM(ps_acc[1], ps_acc[1].ap, ones_b.ap, sq.ap, i == 0, i == 15, [ones_b, sq], True)
            TS(hg_v[:, i, :], hb_v[:, i, :], gcol(G_MLP + i), ALU.mult, [hbuf, gc], [hg])
            if i == 15:
                rsq(rstd2, rstd2.ap, ps_acc[1].ap, [ps_acc[1]])
        proj16(w_out, 0, 16, c_wo, mg_v, merged)

        wtask(None, lambda ws: fence())
        for qd in range(4):
            def c_up(i, ps):
                t = tmpf.next()
                STT(t.ap, ps.ap, 0.0, rstd2.ap, ALU.max, ALU.mult, [ps, rstd2], [t])
                ACT(aq_v[:, i, :], t.ap, AF.Square, [t], [aq])
            proj16(w_up, qd * 2048, 16, c_up, hg_v, hg)

            def c_dn(i, ps):
                TT(hb_v[:, i, :], hb_v[:, i, :], ps.ap, ALU.add, [hbuf, ps], [hbuf])
            proj16(w_down[qd * 2048:(qd + 1) * 2048, :], 0, 16, c_dn, aq_v, aq)

        def t_g(ws, q0=q0):
            for i in range(16):
                sq = sqring.next()
                ACT(sq.ap, hb_v[:, i, :], AF.Square, [hbuf], [sq], scale=float(D ** -0.5))
                MM(ps_acc[1], ps_acc[1].ap, ones_b.ap, sq.ap, i == 0, i == 15, [ones_b, sq], True)
            rsq(rstd2, rstd2.ap, ps_acc[1].ap, [ps_acc[1]])
            for i in range(16):
                ys = yring.next()
                STT(ys.ap, hb_v[:, i, :], gcol(G_FIN + i), rstd2.ap, ALU.mult, ALU.mult, [hbuf, gc, rstd2], [ys])
                T.dma("act", yT[i * 128:(i + 1) * 128, q0:q0 + 512], ys.ap, reads=[ys], sem_buf=ys)
        wtask(None, t_g)


    KM_v = KM.ap.rearrange("p (j h t) -> p j h t", j=2, h=4)
    VM_v = VM.ap.rearrange("p (j t n) -> p j t n", j=2, t=2)
    for mj in range(2 if KNOB["mem"] else 0):
        build_xg(memT, mj * 256, 256, G_MEM)
        tok_rstd(2)
        for grp in range(4):
            ws = wring.next()
            wv = ws.ap[:, 0:16 * 256].rearrange("p (k n) -> p k n", n=256)
            T.dma("pool", wv, w_mem_kv[:, grp * 256:(grp + 1) * 256].rearrange("(k p) n -> p k n", p=128),
                  writes=[ws], sem_buf=ws)
            if grp < 2:
                for jl in range(2):
                    h = grp * 2 + jl
                    ps = ps_ring.next()
                    for kc in range(16):
                        MM(ps, ps.ap[:, 0:256], wv[:, kc, jl * 128:(jl + 1) * 128], xg_v[:, kc, 0:256], kc == 0, kc == 15,
                           [ws, xg], kc == 15)
                    TT(KM_v[:, mj, h, :], ps.ap[:, 0:256], rstd_x.ap[:, 0:256], ALU.mult, [ps, rstd_x], [KM])
            else:
                for tt in range(2):
                    ps = ps_ring.next()
                    for kc in range(16):
                        MM(ps, ps.ap[:, 0:256], xg_v[:, kc, tt * 128:(tt + 1) * 128], wv[:, kc, :], kc == 0, kc == 15,
                           [ws, xg], kc == 15)
                    TS(VM_v[:, mj, tt, (grp - 2) * 256:(grp - 1) * 256], ps.ap[:, 0:256], rstd_t.ap[:, tt:tt + 1], ALU.mult,
                       [ps, rstd_t], [VM])


    make_job(0)
    ltasks = [t for t in tasks if t[0] is not None]
    tasks.clear()
    Wc = nc.dram_tensor("Wc", [len(ltasks), 128, SLOTW], BF16).ap()
    Wc_bufs = [Buf("Wc%d" % i) for i in range(len(ltasks))]

    def wstore(i, ws):
        n = nused_tab[i]
        T.dma("pool", Wc[i, :, 0:n], ws.ap[:, 0:n], reads=[ws], writes=[Wc_bufs[i]], sem_buf=ws)

    apos[0] = max(apos[0], s1_end)
    cring = wring
    N1 = min(KNOB["n1"], len(ltasks))

    tail_slot = []

    def convert_range(i0, i1, ring):
        WMODE[0] = "convert"
        prev = None
        for i in range(i0, i1):
            if tail_slot and i >= i1 - KNOB["ntail"]:
                if prev is not None:
                    wstore(*prev)
                    prev = None
                ws = tail_slot[0]
                ltasks[i][0](ws)
                wstore(i, ws)
                continue
            ws = ring.next()
            ltasks[i][0](ws)
            if prev is not None:
                wstore(*prev)
            prev = (i, ws)
        if prev is not None:
            wstore(*prev)
        WMODE[0] = "cached"

    if KNOB["njob"] > 0:
        convert_range(0, N1, wring)
    assert xring.bufs[1].off == xring.bufs[0].off + 2048
    cvX = Buf("cvX", arena[:, xring.bufs[0].off:xring.bufs[0].off + SLOTW // 2].bitcast(BF16))
    cvX.aliases = list(xring.bufs)
    gl = gates + [rstd2] + yring.bufs
    assert all(gl[k + 1].off == gl[k].off + 512 for k in range(5))
    cvG = Buf("cvG", arena[:, gl[0].off:gl[0].off + SLOTW // 2].bitcast(BF16))
    cvG.aliases = gl
    cv_ring = Ring([cvX, cvG])
    conv2.append(lambda: (tail_slot.append(cvX), convert_range(N1, len(ltasks), cv_ring)))
    WMODE[0] = "cached"
    apos[0] = max(apos[0], s1_end)

    cur["xring"] = xring_s1
    for cb in range(KNOB["nctx"]):
        c0 = cb * 512
        cur.update(CUR[cb % 2])
        xgb, rx, rx2, rt = cur["xg"], cur["rstd_x"], cur["rstd_x2"], cur["rstd_t"]
        xgv = xgb.ap.rearrange("p (k n) -> p k n", n=512)
        build_xg(xc, c0, 512, G_MIX)
        T.dma("sp", tabA.ap.rearrange("p (a n) -> p a n", n=512), tabA_c[:, :, c0:c0 + 512], writes=[tabA], sem_buf=tabA)
        T.dma("sp", tabB.ap.rearrange("p (a n) -> p a n", n=512), tabB_c[:, :, c0:c0 + 512], writes=[tabB], sem_buf=tabB)
        qns = []
        for h in range(2):
            ps = ps_ring.next()
            for kc in range(16):
                MM(ps, ps.ap, wkv_v[:, kc, h * 128:(h + 1) * 128], xgv[:, kc, :], kc == 0, kc == 15, [wkv, xgb], kc == 15)
            qns.append(headnorm_part1(ps, G_KA))
        tok_rstd(4)
        va_v = va_st.ap.rearrange("p (h t d) -> p h t d", h=2, t=4)
        for tt in range(4):
            ps = ps_ring.next()
            for kc in range(16):
                MM(ps, ps.ap[:, 0:256], xgv[:, kc, tt * 128:(tt + 1) * 128], wkv_v[:, kc, 256:512], kc == 0, kc == 15,
                   [wkv, xgb], kc == 15)
            TS(va_v[:, :, tt, :], ps.ap[:, 0:256].rearrange("p (h d) -> p h d", h=2), rt.ap[:, tt:tt + 1], ALU.mult,
               [ps, rt], [va_st])
        T.dma("act", VA[:, :, 4 * cb:4 * cb + 4, :].rearrange("h p t d -> p h t d"), va_v,
              reads=[va_st], writes=[VA_b], sem_buf=va_st)
        sqs = []
        for i in range(2):
            ps = ps_ring.next()
            for kc in range(16):
                MM(ps, ps.ap, wkv_v[:, kc, 512 + i * 128:512 + (i + 1) * 128], xgv[:, kc, :], kc == 0, kc == 15,
                   [wkv, xgb], kc == 15)
            sq = sqring.next()
            ACT(sq.ap, ps.ap, AF.Square, [ps], [sq], scale=float(256 ** -0.5))
            ACT(craw.ap[:, i * 512:(i + 1) * 512], ps.ap, AF.Copy, [ps], [craw])
            sqs.append(sq)
        for h in range(2):
            rope(qns[h], qns[h].ap, tabA, tabA.ap[:, 0:512], tabA.ap[:, 512:1024], RA, ka_st.ap[:, h * 512:(h + 1) * 512], [ka_st])
        T.dma("act", KA[:, :, c0:c0 + 512].rearrange("h d t -> d h t"), ka_st.ap.rearrange("p (h t) -> p h t", h=2),
              reads=[ka_st], writes=[KA_b], sem_buf=ka_st)
        pss = ps_ring.next()
        for i in range(2):
            MM(pss, pss.ap, ones_b.ap, sqs[i].ap, i == 0, i == 1, [ones_b, sqs[i]], i == 1)
        u = tmpf.next()
        TT(u.ap, pss.ap, rx2.ap, ALU.mult, [pss, rx2], [u])
        rsq(u, u.ap, u.ap, [u])
        TT(u.ap, u.ap, rx.ap, ALU.mult, [u, rx], [u])
        for i in range(2):
            STT(ckvn.ap[:, i * 512:(i + 1) * 512], craw.ap[:, i * 512:(i + 1) * 512], gcol(G_CKV + i), u.ap, ALU.mult, ALU.mult,
                [craw, gc, u], [ckvn])
        ps = ps_ring.next()
        for kc in range(16):
            MM(ps, ps.ap, wkv_v[:, kc, 768:896], xgv[:, kc, :], kc == 0, kc == 15, [wkv, xgb], kc == 15)
        kr = tmpb.next()
        TT(kr.ap, ps.ap, rx.ap, ALU.mult, [ps, rx], [kr])
        ckvn_v = ckvn.ap.rearrange("p (k n) -> p k n", n=512)
        for h in range(8):
            ps = ps_ring.next()
            for kc in range(2):
                MM(ps, ps.ap, wkvb_h[:, kc, h, 0:128], ckvn_v[:, kc, :], kc == 0, kc == 1, [wkvb, ckvn], kc == 1)
            cp_evac(h, kb_st.ap[:, h * 512:(h + 1) * 512], ps.ap, [ps], [kb_st])
            if h == 1:
                rope(kr, kr.ap, tabB, tabB.ap[:, 0:512], tabB.ap[:, 512:1024], RB, kpe_st.ap, [kpe_st])
                TS(kpe2_st.ap[:, 0:512], kpe_st.ap, gcol(G_MLO), ALU.mult, [kpe_st, gc], [kpe2_st])
                TS(kpe2_st.ap[:, 512:1024], kpe_st.ap, gcol(G_MHI), ALU.mult, [kpe_st, gc], [kpe2_st])
                T.dma("act", KPE[:, :, c0:c0 + 512].rearrange("v p t -> p v t"), kpe2_st.ap.rearrange("p (v t) -> p v t", v=2),
                      reads=[kpe2_st], writes=[KPE_b], sem_buf=kpe2_st)
        T.dma("act", KB[:, :, c0:c0 + 512].rearrange("h d t -> d h t"), kb_st.ap.rearrange("p (h t) -> p h t", h=8),
              reads=[kb_st], writes=[KB_b], sem_buf=kb_st)
        vb_v = vb_st.ap.rearrange("p (h t d) -> p h t d", h=8, t=4)
        for tt in range(4):
            for hgi in range(2):
                ps = ps_ring.next()
                for kc in range(2):
                    MM(ps, ps.ap.rearrange("p (h d) -> p h d", h=4), ckvn_v[:, kc, tt * 128:(tt + 1) * 128],
                       wkvb_h[:, kc, 4 * hgi:4 * hgi + 4, 128:256], kc == 0, kc == 1, [wkvb, ckvn], kc == 1)
                cp_evac(tt * 2 + hgi + 1, vb_v[:, 4 * hgi:4 * hgi + 4, tt, :], ps.ap.rearrange("p (h d) -> p h d", h=4), [ps], [vb_st])
        T.dma("act", VB[:, :, 4 * cb:4 * cb + 4, :].rearrange("h p t d -> p h t d"), vb_v,
              reads=[vb_st], writes=[VB_b], sem_buf=vb_st)
    cur.update(CUR[0])
    cur["xring"] = xring

    fence()
    for jb in range(KNOB["njob"]):
        make_job(jb)
    run_tasks()
    T.final_wait("sp")
    T.emit()
    return nc


def _rope_tables(rows, cols, kind):
    d = np.arange(128)
    if kind == "A":
        part = d // 64
        i = d % 32
        f = (10000.0 ** (-(np.arange(0, 64, 2, dtype=np.float32)) / 64)).astype(np.float32)[i]
        first = (d % 64) < 32
    else:
        dd = d % 64
        part = dd // 32
        i = dd % 16
        f = (10000.0 ** (-(np.arange(0, 32, 2, dtype=np.float32)) / 32)).astype(np.float32)[i]
        first = (dd % 32) < 16
    pos = np.where(part[:, None] == 0, rows[None, :], cols[None, :]).astype(np.float32)
    ang = (pos * f[:, None]).astype(np.float32)
    c = np.cos(ang.astype(np.float64))
    s = np.sin(ang.astype(np.float64)) * np.where(first, -1.0, 1.0)[:, None]
    return np.ascontiguousarray(np.stack([c, s], axis=1).astype(np.float32))


def _rmats():
    d = np.arange(128)
    pa = np.where((d % 64) < 32, d + 32, d - 32)
    pb = np.where((d % 32) < 16, d + 16, d - 16)
    R = np.zeros((128, 256), np.float32)
    R[pa, d] = 1.0
    R[pb, 128 + d] = 1.0
    return R


_NC_CACHE = {}


def prep_inputs(x_prompt, x_sample, mem_prompt, mem_sample, g_mix, w_in, g_qa, g_ka, g_cq, w_q_b, g_ckv, w_kv_b,
                g_mem, w_mem_kv, w_br_a, w_br_b, w_br_m, w_out, g_mlp, w_up, w_down, g_final, cores=range(NCORE)):
    f = lambda a: np.ascontiguousarray(np.asarray(a, dtype=np.float32))
    x_prompt, x_sample, mem_prompt, mem_sample = f(x_prompt), f(x_sample), f(mem_prompt), f(mem_sample)
    gcols = np.zeros((128, NG), np.float32)
    gcols[:, G_MIX:G_MIX + 16] = f(g_mix).reshape(16, 128).T
    gcols[:, G_MLP:G_MLP + 16] = f(g_mlp).reshape(16, 128).T
    gcols[:, G_FIN:G_FIN + 16] = f(g_final).reshape(16, 128).T
    gcols[:, G_MEM:G_MEM + 16] = f(g_mem).reshape(16, 128).T
    gcols[:, G_QA] = f(g_qa).reshape(128)
    gcols[:, G_KA] = f(g_ka).reshape(128)
    gcols[:, G_CQ:G_CQ + 4] = f(g_cq).reshape(4, 128).T
    gcols[:, G_CKV:G_CKV + 2] = f(g_ckv).reshape(2, 128).T
    gcols[:, G_EPS] = EPS
    gcols[0:64, G_MLO] = 1.0
    gcols[64:128, G_MHI] = 1.0
    shared = {
        "rmat": _rmats(), "gcols": gcols,
        "w_in": f(w_in)[0], "w_q_b": f(w_q_b)[0], "w_kv_b": f(w_kv_b)[0], "w_mem_kv": f(w_mem_kv)[0],
        "w_br_a": f(w_br_a)[0], "w_br_b": f(w_br_b)[0], "w_br_m": f(w_br_m)[0], "w_out": f(w_out)[0],
        "w_up": f(w_up)[0], "w_down": f(w_down)[0],
    }
    tp = np.arange(LP)
    ts = np.arange(LS)
    in_maps = []
    xpT = np.ascontiguousarray(x_prompt[0].T)
    for c in cores:
        b, hf = c // 2, c % 2
        xsT = x_sample[b].T
        qp = np.arange(1024 * c, 1024 * (c + 1))
        qs = np.arange(1024 * hf, 1024 * (hf + 1))
        m = dict(shared)
        m["xq"] = np.ascontiguousarray(np.concatenate([xpT[:, qp], xsT[:, qs]], axis=1))
        m["xc"] = np.ascontiguousarray(np.concatenate([xpT, xsT], axis=1))
        m["memT"] = np.ascontiguousarray(np.concatenate([mem_prompt[0].T, mem_sample[b].T], axis=1))
        crow = np.concatenate([tp // 64, ts // 64])
        ccol = np.concatenate([tp % 64, ts % 64])
        qrow = np.concatenate([qp // 64, qs // 64])
        qcol = np.concatenate([qp % 64, qs % 64])
        m["tabA_c"] = _rope_tables(crow, ccol, "A")
        m["tabB_c"] = _rope_tables(crow, ccol, "B")
        m["tabA_q"] = _rope_tables(qrow, qcol, "A")
        m["tabB_q"] = _rope_tables(qrow, qcol, "B")
        in_maps.append(m)
    return in_maps


def kernel(**inputs):
    in_maps = prep_inputs(**inputs)
    if "nc" not in _NC_CACHE:
        _NC_CACHE["nc"] = build_program()
    nc = _NC_CACHE["nc"]
    res = run_bass_kernel_spmd(nc, in_maps, core_ids=list(range(NCORE)))
    y_prompt = np.empty((1, LP, D), np.float32)
    y_sample = np.empty((4, LS, D), np.float32)
    for c in range(NCORE):
        b, hf = c // 2, c % 2
        yt = res.results[c]["yT"]
        y_prompt[0, 1024 * c:1024 * (c + 1), :] = yt[:, 0:1024].T
        y_sample[b, 1024 * hf:1024 * (hf + 1), :] = yt[:, 1024:2048].T
    return (y_prompt, y_sample)
```
